# Optimizing a Trainium2 kernel written in Bass

```python
import math
import jax, jax.numpy as jnp
from jax import lax
import numpy as np

D_MODEL = 1024
BATCH = 8
SEQ = 4096
DEPTH = 4

GRID_W = 64
CTX_LEN = 256
N_MIXERS = 3
N_MOD = 9
ROPE_THETA = 10000.0
NORM_EPS = 1e-6
Q_BLOCK = 128
NEG_INF = -1e30
D_FF = ((8 * D_MODEL // 3 + 127) // 128) * 128

A_HEADS = D_MODEL // 128
A_NOPE = 128
A_ROPE = 64
A_V = 128
A_Q_LORA = 3 * D_MODEL // 8
A_KV_LORA = D_MODEL // 4
A_IN = A_Q_LORA + A_KV_LORA + A_ROPE

B_HEAD_DIM = 64
B_HEADS = D_MODEL // (2 * B_HEAD_DIM)
B_WIDTH = B_HEADS * 2 * B_HEAD_DIM

C_HEAD_DIM = 64
C_HEADS = D_MODEL // C_HEAD_DIM
C_KV_HEADS = C_HEADS // 4
C_GROUP = C_HEADS // C_KV_HEADS
C_WINDOW = 128

N_A = len(range(0, DEPTH, N_MIXERS))
N_B = len(range(1, DEPTH, N_MIXERS))
N_C = len(range(2, DEPTH, N_MIXERS))

kernel_name = "hybrid_dit_mla_diff_swa_macaron"


def rmsnorm(x, g):
    xf = x.astype(jnp.float32)
    xf = xf * lax.rsqrt(jnp.mean(jnp.square(xf), axis=-1, keepdims=True) + NORM_EPS)
    return (xf * g.astype(jnp.float32)).astype(x.dtype)


def modulate(h, shift, scale):
    return h * (1.0 + scale) + shift


def swiglu(h, w_in, w_out):
    gate, up = jnp.split(h @ w_in, 2, axis=-1)
    return (jax.nn.silu(gate) * up) @ w_out


def ffn_half_step(x, g, shift, scale, gate, w_in, w_out):
    return x + 0.5 * gate * swiglu(modulate(rmsnorm(x, g), shift, scale), w_in, w_out)


def merge_heads(o):
    Bn, H, L, d = o.shape
    return o.transpose(0, 2, 1, 3).reshape(Bn, L, H * d)


def softmax_f32(s, scale):
    return jax.nn.softmax(s.astype(jnp.float32) * scale, axis=-1)


def axial_rope_tables(row, col, dim, dtype):
    quarter = dim // 4
    inv_freq = ROPE_THETA ** (-jnp.arange(quarter, dtype=jnp.float32) / quarter)
    ang = jnp.stack([row.astype(jnp.float32)[:, None] * inv_freq,
                     col.astype(jnp.float32)[:, None] * inv_freq], axis=1)
    return jnp.cos(ang).astype(dtype), jnp.sin(ang).astype(dtype)


def apply_axial_rope(x, cos, sin):
    q = x.shape[-1] // 4
    xs = x.reshape(*x.shape[:-1], 2, 2, q)
    x1, x2 = xs[..., 0, :], xs[..., 1, :]
    out = jnp.stack([x1 * cos - x2 * sin, x2 * cos + x1 * sin], axis=-2)
    return out.reshape(x.shape)


def sweep_query_blocks(fn, q):
    *lead, S, d = q.shape
    nb = S // Q_BLOCK
    blocks = jnp.moveaxis(q.reshape(*lead, nb, Q_BLOCK, d), -3, 0)
    out = jnp.moveaxis(lax.map(fn, blocks), 0, -3)
    return out.reshape(*out.shape[:-3], nb * Q_BLOCK, out.shape[-1])


def mla_mixer(h_lat, h_ctx, row, col, w_in, g_q, g_kv, w_qb, w_kvb, w_o, need_ctx_out):
    scale = (A_NOPE + A_ROPE) ** -0.5
    cos, sin = axial_rope_tables(row, col, A_ROPE, h_lat.dtype)

    def compress(h):
        a = h @ w_in
        return (a[..., :A_Q_LORA],
                rmsnorm(a[..., A_Q_LORA:A_Q_LORA + A_KV_LORA], g_kv),
                a[..., A_Q_LORA + A_KV_LORA:])

    def queries(cq):
        Bn, L, _ = cq.shape
        return (rmsnorm(cq, g_q) @ w_qb).reshape(Bn, L, A_HEADS, A_NOPE + A_ROPE).transpose(0, 2, 1, 3)

    def keys_values(ckv, k_rope):
        Bn, L, _ = ckv.shape
        kv = (ckv @ w_kvb).reshape(Bn, L, A_HEADS, A_NOPE + A_V).transpose(0, 2, 1, 3)
        k_rope = jnp.broadcast_to(k_rope[:, None], (Bn, A_HEADS, L, A_ROPE))
        return jnp.concatenate([kv[..., :A_NOPE], k_rope], axis=-1), kv[..., A_NOPE:]

    cq_l, ckv_l, kr_l = compress(h_lat)
    cq_c, ckv_c, kr_c = compress(h_ctx)
    q_l = queries(cq_l)
    q_l = jnp.concatenate([q_l[..., :A_NOPE], apply_axial_rope(q_l[..., A_NOPE:], cos, sin)], axis=-1)
    k_l, v_l = keys_values(ckv_l, apply_axial_rope(kr_l, cos, sin))
    k_c, v_c = keys_values(ckv_c, kr_c)
    k_all = jnp.concatenate([k_c, k_l], axis=2)
    v_all = jnp.concatenate([v_c, v_l], axis=2)

    def attend(qb):
        p = softmax_f32(jnp.einsum('bhqd,bhkd->bhqk', qb, k_all), scale)
        return jnp.einsum('bhqk,bhkd->bhqd', p.astype(v_all.dtype), v_all)

    y_lat = merge_heads(sweep_query_blocks(attend, q_l)) @ w_o
    y_ctx = None
    if need_ctx_out:
        q_c = queries(cq_c)
        p = softmax_f32(jnp.einsum('bhqd,bhkd->bhqk', q_c, k_c), scale)
        y_ctx = merge_heads(jnp.einsum('bhqk,bhkd->bhqd', p.astype(v_c.dtype), v_c)) @ w_o
    return y_lat, y_ctx


def diff_mixer(h_lat, h_ctx, row, col, w_qkv, lam_params, g_sub, w_o, layer_idx, need_ctx_out):
    d = B_HEAD_DIM
    scale = d ** -0.5
    lam_init = 0.8 - 0.6 * math.exp(-0.3 * layer_idx)
    lp = lam_params.astype(jnp.float32)
    lam = jnp.exp(jnp.sum(lp[0] * lp[1])) - jnp.exp(jnp.sum(lp[2] * lp[3])) + lam_init
    cos, sin = axial_rope_tables(row, col, d, h_lat.dtype)

    def project(h):
        Bn, L, _ = h.shape
        qkv = h @ w_qkv
        q = qkv[..., :B_WIDTH].reshape(Bn, L, B_HEADS, 2, d).transpose(0, 2, 3, 1, 4)
        k = qkv[..., B_WIDTH:2 * B_WIDTH].reshape(Bn, L, B_HEADS, 2, d).transpose(0, 2, 3, 1, 4)
        v = qkv[..., 2 * B_WIDTH:].reshape(Bn, L, B_HEADS, 2 * d).transpose(0, 2, 1, 3)
        return q, k, v

    def diff_attend(q, k, v):
        p = softmax_f32(jnp.einsum('bhmqd,bhmkd->bhmqk', q, k), scale)
        a = p[:, :, 0] - lam * p[:, :, 1]
        return jnp.einsum('bhqk,bhkd->bhqd', a.astype(v.dtype), v)

    def finish(o):
        return merge_heads(rmsnorm(o, g_sub) * (1.0 - lam_init)) @ w_o

    q_l, k_l, v_l = project(h_lat)
    q_c, k_c, v_c = project(h_ctx)
    q_l = apply_axial_rope(q_l, cos, sin)
    k_l = apply_axial_rope(k_l, cos, sin)
    k_all = jnp.concatenate([k_c, k_l], axis=3)
    v_all = jnp.concatenate([v_c, v_l], axis=2)
    y_lat = finish(sweep_query_blocks(lambda qb: diff_attend(qb, k_all, v_all), q_l))
    y_ctx = finish(diff_attend(q_c, k_c, v_c)) if need_ctx_out else None
    return y_lat, y_ctx


def swa_mixer(h_lat, h_ctx, row, col, w_qkv, sink, w_o, need_ctx_out):
    d = C_HEAD_DIM
    scale = d ** -0.5
    S = h_lat.shape[1]
    cos, sin = axial_rope_tables(row, col, d, h_lat.dtype)
    nq, nkv = C_HEADS * d, C_KV_HEADS * d

    def project(h):
        Bn, L, _ = h.shape
        qkv = h @ w_qkv
        q = qkv[..., :nq].reshape(Bn, L, C_KV_HEADS, C_GROUP, d).transpose(0, 2, 3, 1, 4)
        k = qkv[..., nq:nq + nkv].reshape(Bn, L, C_KV_HEADS, d).transpose(0, 2, 1, 3)
        v = qkv[..., nq + nkv:].reshape(Bn, L, C_KV_HEADS, d).transpose(0, 2, 1, 3)
        return q, k, v

    sink_logit = sink.astype(jnp.float32).reshape(C_KV_HEADS, C_GROUP)[None, :, :, None, None]

    def sink_softmax(logits):
        sl = jnp.broadcast_to(sink_logit, logits.shape[:-1] + (1,))
        return jax.nn.softmax(jnp.concatenate([sl, logits], axis=-1), axis=-1)[..., 1:]

    def merge(o):
        Bn, _, _, L, _ = o.shape
        return o.transpose(0, 3, 1, 2, 4).reshape(Bn, L, nq) @ w_o

    q_l, k_l, v_l = project(h_lat)
    q_c, k_c, v_c = project(h_ctx)
    q_l = apply_axial_rope(q_l, cos, sin)
    k_l = apply_axial_rope(k_l, cos, sin)
    n_ctx = k_c.shape[2]
    pad = ((0, 0), (0, 0), (Q_BLOCK, Q_BLOCK), (0, 0))
    k_pad = jnp.pad(k_l, pad)
    v_pad = jnp.pad(v_l, pad)
    offs_q = jnp.arange(Q_BLOCK, dtype=jnp.int32)
    offs_k = jnp.arange(3 * Q_BLOCK, dtype=jnp.int32) - Q_BLOCK

    def band_block(start):
        qb = lax.dynamic_slice_in_dim(q_l, start, Q_BLOCK, axis=3)
        kb = lax.dynamic_slice_in_dim(k_pad, start, 3 * Q_BLOCK, axis=2)
        vb = lax.dynamic_slice_in_dim(v_pad, start, 3 * Q_BLOCK, axis=2)
        key_pos = start + offs_k
        valid = ((jnp.abs(offs_k[None, :] - offs_q[:, None]) <= C_WINDOW)
                 & (key_pos >= 0)[None, :] & (key_pos < S)[None, :])
        s_ctx = jnp.einsum('bhgqd,bhkd->bhgqk', qb, k_c).astype(jnp.float32) * scale
        s_win = jnp.where(valid, jnp.einsum('bhgqd,bhkd->bhgqk', qb, kb).astype(jnp.float32) * scale, NEG_INF)
        p = sink_softmax(jnp.concatenate([s_ctx, s_win], axis=-1)).astype(v_c.dtype)
        return (jnp.einsum('bhgqk,bhkd->bhgqd', p[..., :n_ctx], v_c)
                + jnp.einsum('bhgqk,bhkd->bhgqd', p[..., n_ctx:], vb))

    o = lax.map(band_block, jnp.arange(S // Q_BLOCK, dtype=jnp.int32) * Q_BLOCK)
    o = jnp.moveaxis(o, 0, 3)
    o = o.reshape(*o.shape[:3], S, d)
    y_lat = merge(o)
    y_ctx = None
    if need_ctx_out:
        p = sink_softmax(jnp.einsum('bhgqd,bhkd->bhgqk', q_c, k_c).astype(jnp.float32) * scale)
        y_ctx = merge(jnp.einsum('bhgqk,bhkd->bhgqd', p.astype(v_c.dtype), v_c))
    return y_lat, y_ctx


def setup_inputs(seed: int = 0) -> dict:
    key = jax.random.key(seed)
    ks = jax.random.split(key, 24)

    def nrm(k, shape, s):
        return jax.random.normal(k, shape, jnp.float32) * s

    def gain(k, shape):
        return 1.0 + 0.05 * jax.random.normal(k, shape, jnp.float32)

    return {
        "x": nrm(ks[0], (BATCH, SEQ, D_MODEL), 1.0),
        "c": nrm(ks[1], (BATCH, D_MODEL), 1.0),
        "ctx": nrm(ks[2], (BATCH, CTX_LEN, D_MODEL), 1.0),
        "c_ctx": nrm(ks[3], (D_MODEL,), 1.0),
        "w_mod": nrm(ks[4], (DEPTH, D_MODEL, N_MOD * D_MODEL), 0.5 * D_MODEL ** -0.5),
        "b_mod": nrm(ks[5], (DEPTH, N_MOD * D_MODEL), 0.02),
        "g_norm": gain(ks[6], (DEPTH, 3, D_MODEL)),
        "w_ffn_in": nrm(ks[7], (DEPTH, 2, D_MODEL, 2 * D_FF), D_MODEL ** -0.5),
        "w_ffn_out": nrm(ks[8], (DEPTH, 2, D_FF, D_MODEL), D_FF ** -0.5),
        "a_w_in": nrm(ks[9], (N_A, D_MODEL, A_IN), D_MODEL ** -0.5),
        "a_g_q": gain(ks[10], (N_A, A_Q_LORA)),
        "a_g_kv": gain(ks[11], (N_A, A_KV_LORA)),
        "a_w_qb": nrm(ks[12], (N_A, A_Q_LORA, A_HEADS * (A_NOPE + A_ROPE)), A_Q_LORA ** -0.5),
        "a_w_kvb": nrm(ks[13], (N_A, A_KV_LORA, A_HEADS * (A_NOPE + A_V)), A_KV_LORA ** -0.5),
        "a_w_o": nrm(ks[14], (N_A, A_HEADS * A_V, D_MODEL), (A_HEADS * A_V) ** -0.5),
        "b_w_qkv": nrm(ks[15], (N_B, D_MODEL, 3 * B_WIDTH), D_MODEL ** -0.5),
        "b_lambda": nrm(ks[16], (N_B, 4, B_HEAD_DIM), 0.1),
        "b_g_sub": gain(ks[17], (N_B, 2 * B_HEAD_DIM)),
        "b_w_o": nrm(ks[18], (N_B, B_WIDTH, D_MODEL), B_WIDTH ** -0.5),
        "c_w_qkv": nrm(ks[19], (N_C, D_MODEL, (C_HEADS + 2 * C_KV_HEADS) * C_HEAD_DIM), D_MODEL ** -0.5),
        "c_sink": nrm(ks[20], (N_C, C_HEADS), 1.0),
        "c_w_o": nrm(ks[21], (N_C, C_HEADS * C_HEAD_DIM, D_MODEL), (C_HEADS * C_HEAD_DIM) ** -0.5),
        "g_final": gain(ks[22], (D_MODEL,)),
    }


def reference(x, c, ctx, c_ctx, w_mod, b_mod, g_norm, w_ffn_in, w_ffn_out,
              a_w_in, a_g_q, a_g_kv, a_w_qb, a_w_kvb, a_w_o,
              b_w_qkv, b_lambda, b_g_sub, b_w_o,
              c_w_qkv, c_sink, c_w_o, g_final):
    Bn, S, _ = x.shape
    rows = S // GRID_W
    row = jnp.repeat(jnp.arange(rows, dtype=jnp.int32), GRID_W)
    col = jnp.tile(jnp.arange(GRID_W, dtype=jnp.int32), rows)
    silu_c = jax.nn.silu(c)
    silu_cc = jax.nn.silu(c_ctx)

    for i in range(DEPTH):
        need_ctx = i < DEPTH - 1
        mod_l = (silu_c @ w_mod[i] + b_mod[i]).reshape(Bn, N_MOD, 1, D_MODEL)
        mod_c = (silu_cc @ w_mod[i] + b_mod[i]).reshape(N_MOD, D_MODEL)

        x = ffn_half_step(x, g_norm[i, 0], mod_l[:, 0], mod_l[:, 1], mod_l[:, 2], w_ffn_in[i, 0], w_ffn_out[i, 0])
        ctx = ffn_half_step(ctx, g_norm[i, 0], mod_c[0], mod_c[1], mod_c[2], w_ffn_in[i, 0], w_ffn_out[i, 0])

        h_l = modulate(rmsnorm(x, g_norm[i, 1]), mod_l[:, 3], mod_l[:, 4])
        h_c = modulate(rmsnorm(ctx, g_norm[i, 1]), mod_c[3], mod_c[4])
        kind, j = i % N_MIXERS, i // N_MIXERS
        if kind == 0:
            y_l, y_c = mla_mixer(h_l, h_c, row, col, a_w_in[j], a_g_q[j], a_g_kv[j],
                                 a_w_qb[j], a_w_kvb[j], a_w_o[j], need_ctx)
        elif kind == 1:
            y_l, y_c = diff_mixer(h_l, h_c, row, col, b_w_qkv[j], b_lambda[j], b_g_sub[j], b_w_o[j], i, need_ctx)
        else:
            y_l, y_c = swa_mixer(h_l, h_c, row, col, c_w_qkv[j], c_sink[j], c_w_o[j], need_ctx)
        x = x + mod_l[:, 5] * y_l

        x = ffn_half_step(x, g_norm[i, 2], mod_l[:, 6], mod_l[:, 7], mod_l[:, 8], w_ffn_in[i, 1], w_ffn_out[i, 1])
        if need_ctx:
            ctx = ctx + mod_c[5] * y_c
            ctx = ffn_half_step(ctx, g_norm[i, 2], mod_c[6], mod_c[7], mod_c[8], w_ffn_in[i, 1], w_ffn_out[i, 1])

    return rmsnorm(x, g_final)
```

```python
import math
from contextlib import ExitStack

import numpy as np
import concourse.bass as bass
import concourse.mybir as mybir
from concourse.bass_utils import run_bass_kernel_spmd

F32 = mybir.dt.float32
BF16 = mybir.dt.bfloat16
AF = mybir.ActivationFunctionType
ALU = mybir.AluOpType

D = 1024
NCH = 8
CTX = 256
SEQ = 4096
T = CTX + SEQ
DEPTH = 4
DFF = 2816
NFF = DFF // 128
EPS = 1e-6
TILES = [(0, 256)] + [(256 + 512 * i, 512) for i in range(8)]
DBG = {}
LAST = {}


class Eng:
    def __init__(self, K, name, eng):
        self.K, self.name, self.eng = K, name, eng
        self.sem = K.new_sem("e_" + name)
        self.count = 0
        self.known = {}

    def wait(self, deps):
        for sem, val in deps.items():
            if sem is self.sem and self.name == "pe":
                continue
            if self.known.get(sem, 0) >= val:
                continue
            self.eng.wait_ge(sem, val)
            self.known[sem] = val


class DSem:
    def __init__(self, sem):
        self.sem = sem
        self.total = 0


class Buf:
    def __init__(self, K, name, t):
        self.K, self.name, self.t = K, name, t
        self.w = {}
        self.r = {}
        self.ld = None
        self.st = None

    def __getitem__(self, idx):
        return self.t[idx]


def _merge(d, sem, val):
    if d.get(sem, 0) < val:
        d[sem] = val


class Kern:
    def __init__(self, nc, es):
        self.nc, self.es = nc, es
        self.nsem = 0
        self.uid = 0
        self.free_dsems = {}
        self.live_dsems = []
        self.pe = Eng(self, "pe", nc.tensor)
        self.act = Eng(self, "act", nc.scalar)
        self.dve = Eng(self, "dve", nc.vector)
        self.pool = Eng(self, "pool", nc.gpsimd)
        self.sp = Eng(self, "sp", nc.sync)
        self.engs = [self.pe, self.act, self.dve, self.pool, self.sp]

    def new_sem(self, name):
        self.nsem += 1
        assert self.nsem <= 100, "too many semaphores"
        return self.es.enter_context(self.nc.semaphore(name))

    def get_dsem(self, q):
        kind = "sw" if q.name == "pool" else "hw"
        fl = self.free_dsems.setdefault(kind, [])
        if fl:
            d = fl.pop()
        else:
            d = DSem(self.new_sem(f"dma{kind}{self.nsem}"))
            d.kind = kind
        self.live_dsems.append(d)
        return d

    def sbuf(self, name, shape, dt, es=None):
        self.uid += 1
        name = f"{name}_{self.uid}"
        t = (es or self.es).enter_context(self.nc.sbuf_tensor(name, shape, dt))
        return Buf(self, name, t)

    def psum(self, name, shape, dt, es=None):
        t = (es or self.es).enter_context(self.nc.psum_tensor(name, shape, dt))
        return Buf(self, name, t)

    def op(self, e, fn, reads=(), writes=()):
        deps = {}
        for b in reads:
            for s, v in b.w.items():
                _merge(deps, s, v)
        for b in writes:
            for s, v in b.w.items():
                _merge(deps, s, v)
            for s, v in b.r.items():
                _merge(deps, s, v)
        e.wait(deps)
        ins = fn()
        ins.then_inc(e.sem, 1)
        e.count += 1
        tok = (e.sem, e.count)
        for b in writes:
            b.w = {tok[0]: tok[1]}
            b.r = {}
        for b in reads:
            if b not in writes:
                _merge(b.r, tok[0], tok[1])
        return ins

    def load(self, q, buf, pairs):
        if buf.ld is None:
            buf.ld = self.get_dsem(q)
        assert buf.ld.kind == ("sw" if q.name == "pool" else "hw")
        deps = {}
        for s, v in buf.w.items():
            _merge(deps, s, v)
        for s, v in buf.r.items():
            _merge(deps, s, v)
        q.wait(deps)
        for o, i in pairs:
            q.eng.dma_start(out=o, in_=i).then_inc(buf.ld.sem, 16)
            buf.ld.total += 16
        buf.w = {buf.ld.sem: buf.ld.total}
        buf.r = {}

    def store(self, q, buf, pairs):
        if buf.st is None:
            buf.st = self.get_dsem(q)
        assert buf.st.kind == ("sw" if q.name == "pool" else "hw")
        deps = {}
        for s, v in buf.w.items():
            _merge(deps, s, v)
        q.wait(deps)
        for o, i in pairs:
            q.eng.dma_start(out=o, in_=i).then_inc(buf.st.sem, 16)
            buf.st.total += 16
        _merge(buf.r, buf.st.sem, buf.st.total)

    def barrier(self):
        deps = {}
        for e in self.engs:
            if e.count:
                deps[e.sem] = e.count
        for d in self.live_dsems:
            if d.total:
                deps[d.sem] = d.total
        for e in self.engs:
            e.wait(deps)

    def drop(self, bufs):
        for b in bufs:
            for d in (b.ld, b.st):
                if d is not None:
                    self.live_dsems.remove(d)
                    self.free_dsems[d.kind].append(d)
            b.ld = b.st = None


def build_program(stop_after=None, debug_x=False):
    nc = bass.Bass("TRN2", target_bir_lowering=False)
    es = ExitStack()
    with es:
        _build(nc, es, stop_after, debug_x)
    return nc


def _build(nc, es, stop_after, debug_x):
    K = Kern(nc, es)
    LAST['K'] = K
    pe, act, dve, pool, sp = K.pe, K.act, K.dve, K.pool, K.sp

    def din(name, shape, dt=F32):
        return nc.dram_tensor(name, shape, dt, kind="ExternalInput").ap()

    x_in = din("xT0", [D, T])
    cvec = din("cvec", [128, NCH, 2])
    w_mod = din("w_mod", [DEPTH, D, 9 * D])
    b_mod = din("b_modT", [128, DEPTH * 72])
    g_all = din("g_allT", [128, DEPTH * 3 * NCH + NCH])
    w_ffn_in = din("w_ffn_in", [DEPTH, 2, D, 2 * DFF])
    w_ffn_out = din("w_ffn_out", [DEPTH, 2, DFF, D])
    rope_d = din("rope", [2, 128, T])
    a_win = din("a_win", [2, D, 768])
    a_wq = din("a_wq", [2, 384, 2048])
    a_wkn = din("a_wkn", [2, 256, 1024])
    a_wv = din("a_wv", [2, 256, 1024])
    a_wo = din("a_wo", [2, D, D])
    a_g = din("a_g", [128, 10])
    b_wq = din("b_wq", [D, D])
    b_wqP = din("b_wqP", [D, D])
    b_wk = din("b_wk", [D, D])
    b_wkP = din("b_wkP", [D, D])
    b_wv = din("b_wv", [D, D])
    b_wo = din("b_wo", [D, D])
    b_lam = din("b_lam", [128, 256])
    b_gsub = din("b_gsub", [128, 1])
    c_wq = din("c_wq", [D, D])
    c_wqP = din("c_wqP", [D, D])
    c_wkd = din("c_wkd", [D, 512])
    c_wkdP = din("c_wkdP", [D, 512])
    c_wv = din("c_wv", [D, 256])
    c_wo = din("c_wo", [D, D])
    c_sinkr = din("c_sinkr", [128, 16])
    c_mask = din("c_mask", [2, 128, 512])
    out_d = nc.dram_tensor("outT", [D, SEQ], F32, kind="ExternalOutput").ap()
    NSNAP = stop_after if stop_after else 12
    if debug_x:
        dbg = nc.dram_tensor("dbg", [NSNAP, D, 1280], F32, kind="ExternalOutput").ap()

    def scratch(name, shape, dt=BF16):
        return nc.dram_tensor(name, shape, dt, kind="Internal").ap()

    q1s = scratch("q1s", [D, T])
    q2s = scratch("q2s", [512, T])
    k1s = scratch("k1s", [D, T])
    k2s = scratch("k2s", [64, T])
    kds = scratch("kds", [4, 128, T])
    vs = scratch("vs", [8, 128, 34, 128])
    vcs = scratch("vcs", [128, 34, 256])
    osc = scratch("osc", [D, T])
    xs = nc.dram_tensor("xs", [D, T], F32, kind="Internal").ap()
    hs = nc.dram_tensor("hs", [D, T], BF16, kind="Internal").ap()

    xs_v = xs.rearrange("(k p) t -> p k t", p=128)
    xin_v = x_in.rearrange("(k p) t -> p k t", p=128)
    hs_v = hs.rearrange("(k p) t -> p k t", p=128)

    WSLOT = 34 * 1024
    wslot = [K.sbuf(f"wslot{i}", [128, WSLOT], BF16) for i in range(2)]
    MOD = K.sbuf("MOD", [128, DEPTH * 9 * NCH * 2], F32)
    GALL = K.sbuf("GALL", [128, DEPTH * 3 * NCH + NCH], F32)
    GS = K.sbuf("GS", [128, DEPTH * 3 * NCH * 2], F32)
    GT = K.sbuf("GT", [128, DEPTH * 3 * NCH * 2], F32)
    ones_f = K.sbuf("ones_f", [128, 128], F32)
    ones_b = K.sbuf("ones_b", [128, 128], BF16)
    epsb = K.sbuf("epsb", [128, 1], F32)
    PS = [K.psum(f"ps{i}", [128, 512], F32) for i in range(8)]

    K.op(dve, lambda: nc.vector.memset(ones_f[:, :], 1.0), writes=[ones_f])
    K.op(dve, lambda: nc.vector.memset(ones_b[:, :], 1.0), writes=[ones_b])
    K.op(dve, lambda: nc.vector.memset(epsb[:, :], EPS), writes=[epsb])

    def mod_col(layer, m, k, lc):
        i = ((layer * 9 + m) * NCH + k) * 2 + lc
        return MOD[:, i:i + 1]

    def gs_col(layer, n, k, lc):
        i = ((layer * 3 + n) * NCH + k) * 2 + lc
        return GS[:, i:i + 1]

    def gt_col(layer, n, k, lc):
        i = ((layer * 3 + n) * NCH + k) * 2 + lc
        return GT[:, i:i + 1]

    dummy = K.sbuf("dmydst", [128, 2], F32)
    K.load(sp, dummy, [(xs[k * 128:(k + 1) * 128, :], x_in[k * 128:(k + 1) * 128, :]) for k in range(NCH)])

    with ExitStack() as pes:
        cv = K.sbuf("cv", [128, NCH, 2], F32, pes)
        sc = K.sbuf("scv", [128, NCH, 2], F32, pes)
        bm = K.sbuf("bm", [128, DEPTH * 72], F32, pes)
        wm = [K.sbuf(f"wm{i}", [128, NCH, 1024], F32, pes) for i in range(2)]
        K.load(sp, cv, [(cv[:, :, :], cvec)])
        K.load(sp, bm, [(bm[:, :], b_mod)])
        K.load(sp, GALL, [(GALL[:, :], g_all)])
        K.op(act, lambda: nc.scalar.activation(out=sc[:, :, :], in_=cv[:, :, :], func=AF.Silu),
             reads=[cv], writes=[sc])
        it = 0
        for layer in range(DEPTH):
            mp = PS[layer % 2]
            for m in range(9):
                wb = wm[it % 2]
                it += 1
                src = w_mod[layer, :, m * 1024:(m + 1) * 1024].rearrange("(k p) n -> p k n", p=128)
                K.load(sp, wb, [(wb[:, 0:4, :], src[:, 0:4, :]), (wb[:, 4:8, :], src[:, 4:8, :])])
                for c in range(NCH):
                    o = (m * NCH + c) * 2
                    for k in range(NCH):
                        K.op(pe, lambda k=k, c=c, o=o, wb=wb, mp=mp: nc.tensor.matmul(
                            mp[:, o:o + 2], wb[:, k, c * 128:(c + 1) * 128], sc[:, k, :],
                            start=(k == 0), stop=(k == NCH - 1)),
                            reads=[wb, sc], writes=[mp])
            base = layer * 144
            for lc in range(2):
                K.op(dve, lambda lc=lc, mp=mp, base=base, layer=layer: nc.vector.tensor_tensor(
                    out=MOD[:, base + lc:base + 144:2], in0=mp[:, lc:144:2],
                    in1=bm[:, layer * 72:(layer + 1) * 72], op=ALU.add),
                    reads=[mp, bm], writes=[MOD])
        for layer in range(DEPTH):
            for n in range(3):
                for lc in range(2):
                    gsl = slice(((layer * 3 + n) * NCH) * 2 + lc, ((layer * 3 + n + 1) * NCH) * 2, 2)
                    scl = slice(((layer * 9 + 3 * n + 1) * NCH) * 2 + lc, ((layer * 9 + 3 * n + 2) * NCH) * 2, 2)
                    gtl = slice(((layer * 9 + 3 * n + 2) * NCH) * 2 + lc, ((layer * 9 + 3 * n + 3) * NCH) * 2, 2)
                    gin = GALL[:, (layer * 3 + n) * NCH:(layer * 3 + n + 1) * NCH]
                    K.op(dve, lambda gsl=gsl, scl=scl, gin=gin: nc.vector.scalar_tensor_tensor(
                        out=GS[:, gsl], in0=MOD[:, scl], scalar=1.0, in1=gin, op0=ALU.add, op1=ALU.mult),
                        reads=[MOD, GALL], writes=[GS])
                    K.op(dve, lambda gsl=gsl, gtl=gtl, n=n: nc.vector.tensor_scalar(
                        out=GT[:, gsl], in0=MOD[:, gtl], scalar1=(1.0 if n == 1 else 0.5), scalar2=None,
                        op0=ALU.mult),
                        reads=[MOD], writes=[GT])
        K.barrier()
        K.drop([cv, bm] + wm)

    def norm_bufs(pes, final, ntm=2):
        nb = {}
        nb["sq"] = [K.sbuf(f"fnsq{i}", [128, 512], F32, pes) for i in range(2)]
        nb["acc"] = K.sbuf("fnacc", [128, 512], F32, pes)
        nb["rs"] = K.sbuf("fnrs", [128, 512], F32, pes)
        nb["tm"] = [K.sbuf(f"fntm{i}", [128, 512], F32, pes) for i in range(ntm)]
        if final:
            nb["ho"] = [K.sbuf(f"fnho{i}", [128, 512], F32, pes) for i in range(4)]
        else:
            nb["ho"] = [K.sbuf("fnho", [128, NCH, 512], BF16, pes)]
        nb["all"] = nb["sq"] + [nb["acc"], nb["rs"]] + nb["tm"] + nb["ho"]
        return nb

    def norm_sq_chunk(XT, k, n_t, nb):
        ACC = nb["acc"]
        dst = ACC if k == 0 else nb["sq"][k % 2]
        K.op(dve, lambda: nc.vector.tensor_tensor(
            out=dst[:, 0:n_t], in0=XT[:, k, 0:n_t], in1=XT[:, k, 0:n_t], op=ALU.mult), reads=[XT], writes=[dst])
        if k > 0:
            K.op(pool, lambda: nc.gpsimd.tensor_tensor(
                out=ACC[:, 0:n_t], in0=ACC[:, 0:n_t], in1=dst[:, 0:n_t], op=ALU.add),
                reads=[dst], writes=[ACC])

    def norm_fused(XT, t0, n_t, nb, ps, nxt, do_part1=True):
        layer, n, final = nxt
        lc = 1 if t0 < CTX else 0
        ACC, RS = nb["acc"], nb["rs"]
        for k in (range(NCH) if do_part1 else []):
            dst = ACC if k == 0 else nb["sq"][k % 2]
            K.op(pool, lambda k=k, dst=dst: nc.gpsimd.tensor_tensor(
                out=dst[:, 0:n_t], in0=XT[:, k, 0:n_t], in1=XT[:, k, 0:n_t], op=ALU.mult), reads=[XT], writes=[dst])
            if k > 0:
                K.op(pool, lambda dst=dst: nc.gpsimd.tensor_tensor(
                    out=ACC[:, 0:n_t], in0=ACC[:, 0:n_t], in1=dst[:, 0:n_t], op=ALU.add),
                    reads=[dst], writes=[ACC])

        def step0():
            K.op(pe, lambda: nc.tensor.matmul(ps[:, 0:n_t], ones_f[:, :], ACC[:, 0:n_t], start=True, stop=True),
                 reads=[ones_f, ACC], writes=[ps])
            K.op(act, lambda: nc.scalar.activation(out=RS[:, 0:n_t], in_=ps[:, 0:n_t], func=AF.Sqrt,
                                                   bias=epsb[:, 0:1], scale=1.0 / D),
                 reads=[ps, epsb], writes=[RS])
            K.op(dve, lambda: nc.vector.reciprocal(out=RS[:, 0:n_t], in_=RS[:, 0:n_t]), reads=[RS], writes=[RS])

        def stepk(k):
            TM = nb["tm"][k % len(nb["tm"])]
            K.op(dve, lambda: nc.vector.tensor_tensor(
                out=TM[:, 0:n_t], in0=XT[:, k, 0:n_t], in1=RS[:, 0:n_t], op=ALU.mult),
                reads=[XT, RS], writes=[TM])
            if final:
                gcol = GALL[:, DEPTH * 3 * NCH + k:DEPTH * 3 * NCH + k + 1]
                HK = nb["ho"][k % 4]
                K.op(act, lambda: nc.scalar.activation(
                    out=HK[:, 0:n_t], in_=TM[:, 0:n_t], func=AF.Identity, scale=gcol),
                    reads=[TM, GALL], writes=[HK])
                K.store(pool, HK, [(out_d[k * 128:(k + 1) * 128, t0 - CTX:t0 - CTX + n_t], HK[:, 0:n_t])])
            else:
                HO = nb["ho"][0]
                K.op(act, lambda: nc.scalar.activation(
                    out=HO[:, k, 0:n_t], in_=TM[:, 0:n_t], func=AF.Identity,
                    scale=gs_col(layer, n, k, lc), bias=mod_col(layer, 3 * n, k, lc)),
                    reads=[TM, GS, MOD], writes=[HO])

        def stepend():
            if not final:
                HO = nb["ho"][0]
                K.store(pool, HO, [(hs_v[:, :, t0:t0 + n_t], HO[:, :, 0:n_t])])

        return [step0] + [(lambda k=k: stepk(k)) for k in range(NCH)] + [stepend]

    def norm_phase(layer, n, tiles, final=False):
        with ExitStack() as pes:
            xt = [K.sbuf(f"nx{i}", [128, NCH, 512], F32, pes) for i in range(2)]
            sq = [K.sbuf(f"nsq{i}", [128, 512], F32, pes) for i in range(2)]
            acc = [K.sbuf(f"nacc{i}", [128, 512], F32, pes) for i in range(2)]
            rs = [K.sbuf(f"nrs{i}", [128, 512], F32, pes) for i in range(2)]
            tmp = [K.sbuf(f"ntmp{i}", [128, 512], F32, pes) for i in range(2)]
            if final:
                ho = [K.sbuf(f"nho{i}", [128, 512], F32, pes) for i in range(4)]
            else:
                ho = [K.sbuf(f"nho{i}", [128, NCH, 512], BF16, pes) for i in range(2)]
            loc = xt + sq + acc + rs + tmp + ho
            for ti, (t0, n_t) in enumerate(tiles):
                lc = 1 if t0 < CTX else 0
                X, SQ, ACC, RS, HO = xt[ti % 2], sq[ti % 2], acc[ti % 2], rs[ti % 2], ho[ti % 2]
                ps = PS[ti % 2]
                K.load(sp, X, [(X[:, 0:4, 0:n_t], xs_v[:, 0:4, t0:t0 + n_t]),
                               (X[:, 4:8, 0:n_t], xs_v[:, 4:8, t0:t0 + n_t])])
                for k in range(NCH):
                    dst = ACC if k == 0 else SQ
                    K.op(act, lambda k=k, dst=dst, X=X: nc.scalar.activation(
                        out=dst[:, 0:n_t], in_=X[:, k, 0:n_t], func=AF.Square),
                        reads=[X], writes=[dst])
                    if k > 0:
                        K.op(pool, lambda ACC=ACC, SQ=SQ: nc.gpsimd.tensor_tensor(
                            out=ACC[:, 0:n_t], in0=ACC[:, 0:n_t], in1=SQ[:, 0:n_t], op=ALU.add),
                            reads=[SQ], writes=[ACC])
                K.op(pe, lambda ps=ps, ACC=ACC: nc.tensor.matmul(
                    ps[:, 0:n_t], ones_f[:, :], ACC[:, 0:n_t], start=True, stop=True),
                    reads=[ones_f, ACC], writes=[ps])
                K.op(act, lambda ps=ps, RS=RS: nc.scalar.activation(
                    out=RS[:, 0:n_t], in_=ps[:, 0:n_t], func=AF.Sqrt, bias=epsb[:, 0:1], scale=1.0 / D),
                    reads=[ps, epsb], writes=[RS])
                K.op(dve, lambda RS=RS: nc.vector.reciprocal(out=RS[:, 0:n_t], in_=RS[:, 0:n_t]),
                     reads=[RS], writes=[RS])
                for k in range(NCH):
                    TM = tmp[k % 2]
                    K.op(dve, lambda k=k, TM=TM, X=X, RS=RS: nc.vector.tensor_tensor(
                        out=TM[:, 0:n_t], in0=X[:, k, 0:n_t], in1=RS[:, 0:n_t], op=ALU.mult),
                        reads=[X, RS], writes=[TM])
                    if final:
                        gcol = GALL[:, DEPTH * 3 * NCH + k:DEPTH * 3 * NCH + k + 1]
                        HK = ho[k % 4]
                        K.op(act, lambda k=k, TM=TM, HK=HK, gcol=gcol: nc.scalar.activation(
                            out=HK[:, 0:n_t], in_=TM[:, 0:n_t], func=AF.Identity, scale=gcol),
                            reads=[TM, GALL], writes=[HK])
                        K.store(pool, HK, [(out_d[k * 128:(k + 1) * 128, t0 - CTX:t0 - CTX + n_t], HK[:, 0:n_t])])
                    else:
                        K.op(act, lambda k=k, TM=TM, HO=HO: nc.scalar.activation(
                            out=HO[:, k, 0:n_t], in_=TM[:, 0:n_t], func=AF.Identity,
                            scale=gs_col(layer, n, k, lc), bias=mod_col(layer, 3 * n, k, lc)),
                            reads=[TM, GS, MOD], writes=[HO])
                if not final:
                    K.store(pool, HO, [(hs_v[:, :, t0:t0 + n_t], HO[:, :, 0:n_t])])
            K.barrier()
            K.drop(loc)

    HH = NFF // 2
    HW = HH * 128

    def ffn_weight_load(layer, j, half, slot):
        W = wslot[slot]
        win = w_ffn_in[layer, j].rearrange("(k p) n -> p k n", p=128)
        wv = W[:, 0:NCH * 2 * HW].rearrange("p (k n) -> p k n", k=NCH)
        pairs = []
        for kk in range(0, NCH, 2):
            pairs.append((wv[:, kk:kk + 2, 0:HW], win[:, kk:kk + 2, half * HW:(half + 1) * HW]))
            pairs.append((wv[:, kk:kk + 2, HW:2 * HW], win[:, kk:kk + 2, DFF + half * HW:DFF + (half + 1) * HW]))
        wout = w_ffn_out[layer, j, half * HW:(half + 1) * HW, :].rearrange("(c p) n -> p c n", p=128)
        ov = W[:, NCH * 2 * HW:NCH * 2 * HW + HH * D].rearrange("p (c n) -> p c n", c=HH)
        pairs.append((ov[:, 0:6, :], wout[:, 0:6, :]))
        pairs.append((ov[:, 6:HH, :], wout[:, 6:HH, :]))
        K.load(pool, W, pairs)

    def ffn_half_phase(layer, n, half, slot, tiles, nxt=None):
        W = wslot[slot]
        wv = W[:, 0:NCH * 2 * HW].rearrange("p (k n) -> p k n", k=NCH)
        ov = W[:, NCH * 2 * HW:NCH * 2 * HW + HH * D].rearrange("p (c n) -> p c n", c=HH)
        with ExitStack() as pes:
            ht = [K.sbuf(f"fh{i}", [128, NCH, 512], BF16, pes) for i in range(2)]
            sg = [K.sbuf(f"fs{i}", [128, 512], F32, pes) for i in range(2)]
            if nxt is None:
                gt = [K.sbuf(f"fg{i}", [128, HH, 512], BF16, pes) for i in range(2)]
                xc = [K.sbuf(f"fx{i}", [128, 512], F32, pes) for i in range(4)]
                loc = ht + gt + sg + xc
            else:
                gt = [K.sbuf("fg0", [128, HH, 512], BF16, pes)]
                XT = K.sbuf("fxt", [128, NCH, 512], F32, pes)
                nb = norm_bufs(pes, nxt[2], ntm=1)
                loc = ht + gt + sg + [XT] + nb["all"]
            xi = 0
            pending = []
            for ti, (t0, n_t) in enumerate(tiles):
                lc = 1 if t0 < CTX else 0
                H, G = ht[ti % 2], gt[ti % len(gt)]
                K.load(sp, H, [(H[:, :, 0:n_t], hs_v[:, :, t0:t0 + n_t])])
                xt_loaded = False
                for c in range(HH):
                    pg, pu = PS[(c % 2) * 2], PS[(c % 2) * 2 + 1]
                    for k in range(NCH):
                        K.op(pe, lambda k=k, c=c, pg=pg, H=H: nc.tensor.matmul(
                            pg[:, 0:n_t], wv[:, k, c * 128:(c + 1) * 128], H[:, k, 0:n_t],
                            start=(k == 0), stop=(k == NCH - 1)), reads=[W, H], writes=[pg])
                    for k in range(NCH):
                        K.op(pe, lambda k=k, c=c, pu=pu, H=H: nc.tensor.matmul(
                            pu[:, 0:n_t], wv[:, k, HW + c * 128:HW + (c + 1) * 128], H[:, k, 0:n_t],
                            start=(k == 0), stop=(k == NCH - 1)), reads=[W, H], writes=[pu])
                    S = sg[c % 2]
                    K.op(act, lambda pg=pg, S=S: nc.scalar.activation(
                        out=S[:, 0:n_t], in_=pg[:, 0:n_t], func=AF.Silu), reads=[pg], writes=[S])
                    K.op(dve, lambda c=c, pu=pu, S=S, G=G: nc.vector.tensor_tensor(
                        out=G[:, c, 0:n_t], in0=pu[:, 0:n_t], in1=S[:, 0:n_t], op=ALU.mult),
                        reads=[pu, S], writes=[G])
                    for _ in range(2):
                        if pending:
                            pending.pop(0)()
                    if nxt is not None and not pending and not xt_loaded:
                        K.load(sp, XT, [(XT[:, 0:4, 0:n_t], xs_v[:, 0:4, t0:t0 + n_t]),
                                        (XT[:, 4:8, 0:n_t], xs_v[:, 4:8, t0:t0 + n_t])])
                        xt_loaded = True
                for d in range(NCH):
                    py = PS[4 + d % 2]
                    if nxt is None:
                        XC = xc[xi % 4]
                        xi += 1
                        K.load(sp, XC, [(XC[:, 0:n_t], xs[d * 128:(d + 1) * 128, t0:t0 + n_t])])
                    for c in range(HH):
                        K.op(pe, lambda c=c, d=d, py=py, G=G: nc.tensor.matmul(
                            py[:, 0:n_t], ov[:, c, d * 128:(d + 1) * 128], G[:, c, 0:n_t],
                            start=(c == 0), stop=(c == HH - 1)), reads=[W, G], writes=[py])
                    if nxt is None:
                        K.op(dve, lambda d=d, py=py, XC=XC: nc.vector.scalar_tensor_tensor(
                            out=XC[:, 0:n_t], in0=py[:, 0:n_t], scalar=gt_col(layer, n, d, lc), in1=XC[:, 0:n_t],
                            op0=ALU.mult, op1=ALU.add), reads=[py, GT, XC], writes=[XC])
                        K.store(pool, XC, [(xs[d * 128:(d + 1) * 128, t0:t0 + n_t], XC[:, 0:n_t])])
                    else:
                        K.op(dve, lambda d=d, py=py: nc.vector.scalar_tensor_tensor(
                            out=XT[:, d, 0:n_t], in0=py[:, 0:n_t], scalar=gt_col(layer, n, d, lc),
                            in1=XT[:, d, 0:n_t], op0=ALU.mult, op1=ALU.add), reads=[py, GT, XT], writes=[XT])
                        norm_sq_chunk(XT, d, n_t, nb)
                if nxt is not None:
                    if not nxt[2]:
                        K.store(pool, XT, [(xs_v[:, 0:4, t0:t0 + n_t], XT[:, 0:4, 0:n_t]),
                                           (xs_v[:, 4:8, t0:t0 + n_t], XT[:, 4:8, 0:n_t])])
                    pending = norm_fused(XT, t0, n_t, nb, PS[6], nxt, do_part1=False)
            while pending:
                pending.pop(0)()
            K.barrier()
            K.drop(loc)

    def ffn_phase(layer, n, tiles, do_norm, nxt):
        j = 0 if n == 0 else 1
        ffn_weight_load(layer, j, 0, 0)
        ffn_weight_load(layer, j, 1, 1)
        if do_norm:
            norm_phase(layer, n, tiles)
        ffn_half_phase(layer, n, 0, 0, tiles)
        ffn_half_phase(layer, n, 1, 1, tiles, nxt)

    psrot = [0]

    def ps_next():
        psrot[0] += 1
        return PS[psrot[0] % 8]

    evrot = [0]

    def evac(dst_ap, dst_buf, src_ap, src_buf):
        evrot[0] += 1
        if evrot[0] % 2:
            K.op(act, lambda: nc.scalar.activation(out=dst_ap, in_=src_ap, func=AF.Identity),
                 reads=[src_buf], writes=[dst_buf])
        else:
            K.op(dve, lambda: nc.vector.tensor_copy(out=dst_ap, in_=src_ap),
                 reads=[src_buf], writes=[dst_buf])

    def wload(slot, items):
        W = wslot[slot]
        off = 0
        views, pairs = [], []
        for ap, k, n in items:
            v = W[:, off:off + k * n].rearrange("p (k n) -> p k n", k=k)
            src = ap.rearrange("(k p) n -> p k n", p=128)
            step = max(1, 4096 // n)
            for kk in range(0, k, step):
                k2 = min(k, kk + step)
                pairs.append((v[:, kk:k2, :], src[:, kk:k2, :]))
            views.append(v)
            off += k * n
        assert off <= WSLOT
        K.load(pool, W, pairs)
        return views

    def rope_combine(pes_bufs, A, B, RP, P, n_t, dst_ap, dst_buf):
        t1, t2 = pes_bufs
        K.op(dve, lambda: nc.vector.tensor_tensor(out=t1[0:P, 0:n_t], in0=A[0:P, 0:n_t], in1=RP[0:P, 0, 0:n_t],
                                                  op=ALU.mult), reads=[A, RP], writes=[t1])
        K.op(dve, lambda: nc.vector.tensor_tensor(out=t2[0:P, 0:n_t], in0=B[0:P, 0:n_t], in1=RP[0:P, 1, 0:n_t],
                                                  op=ALU.mult), reads=[B, RP], writes=[t2])
        K.op(pool, lambda: nc.gpsimd.tensor_tensor(out=dst_ap, in0=t1[0:P, 0:n_t], in1=t2[0:P, 0:n_t],
                                                   op=ALU.add), reads=[t1, t2], writes=[dst_buf])

    def proj_fm(H, n_t, wv, c0, M, nk, ps, kb=0):
        for k in range(nk):
            K.op(pe, lambda k=k: nc.tensor.matmul(ps[0:M, 0:n_t], wv[:, k, c0:c0 + M], H[:, kb + k, 0:n_t],
                                                  start=(k == 0), stop=(k == nk - 1)),
                 reads=[H] + [wslot[0], wslot[1]], writes=[ps])

    def out_proj_phase(layer, wo, tiles):
        osc_v = osc.rearrange("(k p) t -> p k t", p=128)
        nxt = (layer, 2, False)
        with ExitStack() as pes:
            ot = [K.sbuf(f"po{i}", [128, NCH, 512], BF16, pes) for i in range(1)]
            xts = [K.sbuf(f"pxt{i}", [128, NCH, 512], F32, pes) for i in range(2)]
            nb = norm_bufs(pes, False, ntm=1)
            pending = None
            for ti, (t0, n_t) in enumerate(tiles):
                lc = 1 if t0 < CTX else 0
                O = ot[0]
                XT = xts[ti % 2]
                K.load(sp, O, [(O[:, :, 0:n_t], osc_v[:, :, t0:t0 + n_t])])
                K.load(sp, XT, [(XT[:, 0:4, 0:n_t], xs_v[:, 0:4, t0:t0 + n_t]),
                                (XT[:, 4:8, 0:n_t], xs_v[:, 4:8, t0:t0 + n_t])])
                for d in range(NCH):
                    py = PS[d % 4]
                    proj_fm(O, n_t, wo, d * 128, 128, NCH, py)
                    K.op(dve, lambda d=d, py=py: nc.vector.scalar_tensor_tensor(
                        out=XT[:, d, 0:n_t], in0=py[:, 0:n_t], scalar=gt_col(layer, 1, d, lc), in1=XT[:, d, 0:n_t],
                        op0=ALU.mult, op1=ALU.add), reads=[py, GT, XT], writes=[XT])
                K.store(pool, XT, [(xs_v[:, 0:4, t0:t0 + n_t], XT[:, 0:4, 0:n_t]),
                                   (xs_v[:, 4:8, t0:t0 + n_t], XT[:, 4:8, 0:n_t])])
                for stp in norm_fused(XT, t0, n_t, nb, PS[6], nxt):
                    stp()
            K.barrier()
            K.drop(ot + xts + nb["all"])

    def attn_inner(n_q, chunks, qk_fn, v_fn, scale, e_bufs, ps_s, ps_o, ps_d, acc, dv, mask_fn=None):
        nck = len(chunks)
        LA = len(ps_s) - 1

        def emit_qk(i):
            ps = ps_s[i % len(ps_s)]
            qk_fn(chunks[i], ps)

        def emit_rest(i):
            c = chunks[i]
            ps = ps_s[i % len(ps_s)]
            E = e_bufs[i % len(e_bufs)]
            K.op(act, lambda: nc.scalar.activation(out=E[:, 0:n_q], in_=ps[:, 0:n_q], func=AF.Exp, scale=scale),
                 reads=[ps], writes=[E])
            if mask_fn is not None:
                mk = mask_fn(c)
                if mk is not None:
                    mbuf, map_ = mk
                    K.op(pool, lambda: nc.gpsimd.tensor_tensor(out=E[:, 0:n_q], in0=E[:, 0:n_q], in1=map_,
                                                               op=ALU.mult), reads=[E, mbuf], writes=[E])
            vl, vb = v_fn(c)
            K.op(pe, lambda: nc.tensor.matmul(ps_o[0:dv, 0:n_q], vl, E[:, 0:n_q], start=(i == 0),
                                              stop=(i == nck - 1)), reads=[vb, E], writes=[ps_o])
            K.op(pe, lambda: nc.tensor.matmul(ps_d[0:dv, 0:n_q], ones_b[:, 0:dv], E[:, 0:n_q], start=(i == 0),
                                              stop=(i == nck - 1)), reads=[ones_b, E], writes=[ps_d])

        for i in range(min(LA, nck)):
            emit_qk(i)
        for i in range(nck):
            emit_rest(i)
            if i + LA < nck:
                emit_qk(i + LA)

    def mla_mixer(layer, j, need_ctx):
        win, wq, wkn, wvv, wo = wload(0, [(a_win[j], 8, 768), (a_wq[j], 3, 2048), (a_wkn[j], 2, 1024),
                                          (a_wv[j], 2, 1024), (a_wo[j], 8, 1024)])
        scale = 192.0 ** -0.5
        with ExitStack() as pes:
            ht = [K.sbuf(f"ah{i}", [128, NCH, 512], BF16, pes) for i in range(2)]
            rp = [K.sbuf(f"arp{i}", [128, 2, 512], F32, pes) for i in range(2)]
            ag = K.sbuf("ag", [128, 10], F32, pes)
            sq = [K.sbuf(f"asq{i}", [128, 512], F32, pes) for i in range(2)]
            acc = K.sbuf("aacc", [128, 512], F32, pes)
            rs = K.sbuf("ars", [128, 512], F32, pes)
            tm = [K.sbuf(f"atm{i}", [128, 512], F32, pes) for i in range(2)]
            cn = K.sbuf("acn", [128, 5, 512], BF16, pes)
            st = [K.sbuf(f"ast{i}", [128, 512], BF16, pes) for i in range(8)]
            vst = [K.sbuf(f"avs{i}", [128, 1024], BF16, pes) for i in range(2)]
            loc = ht + rp + [ag, acc, rs, cn] + sq + tm + st + vst
            K.load(sp, ag, [(ag[:, :], a_g)])
            sti = [0]

            def stage_store(ps, P, n_t, dram_ap):
                S = st[sti[0] % 8]
                sti[0] += 1
                evac(S[0:P, 0:n_t], S, ps[0:P, 0:n_t], ps)
                K.store(pool, S, [(dram_ap, S[0:P, 0:n_t])])

            for ti, (t0, n_t) in enumerate(TILES):
                H, RP = ht[ti % 2], rp[ti % 2]
                K.load(sp, H, [(H[:, :, 0:n_t], hs_v[:, :, t0:t0 + n_t])])
                K.load(sp, RP, [(RP[:, 0, 0:n_t], rope_d[0, :, t0:t0 + n_t]), (RP[:, 1, 0:n_t], rope_d[1, :, t0:t0 + n_t])])
                for (c_lo, nchk, width, gofs, cofs) in ((0, 3, 384, j * 5, 0), (384, 2, 256, j * 5 + 3, 3)):
                    pcs = []
                    for c in range(nchk):
                        ps = ps_next()
                        proj_fm(H, n_t, win, c_lo + c * 128, 128, NCH, ps)
                        pcs.append(ps)
                        dst = acc if c == 0 else sq[c % 2]
                        K.op(act, lambda ps=ps, dst=dst: nc.scalar.activation(
                            out=dst[:, 0:n_t], in_=ps[:, 0:n_t], func=AF.Square), reads=[ps], writes=[dst])
                        if c > 0:
                            K.op(pool, lambda dst=dst: nc.gpsimd.tensor_tensor(
                                out=acc[:, 0:n_t], in0=acc[:, 0:n_t], in1=dst[:, 0:n_t], op=ALU.add),
                                reads=[dst], writes=[acc])
                    pr = ps_next()
                    K.op(pe, lambda pr=pr: nc.tensor.matmul(pr[:, 0:n_t], ones_f[:, :], acc[:, 0:n_t], start=True,
                                                            stop=True), reads=[ones_f, acc], writes=[pr])
                    K.op(act, lambda pr=pr, width=width: nc.scalar.activation(
                        out=rs[:, 0:n_t], in_=pr[:, 0:n_t], func=AF.Sqrt, bias=epsb[:, 0:1], scale=1.0 / width),
                        reads=[pr, epsb], writes=[rs])
                    K.op(dve, lambda: nc.vector.reciprocal(out=rs[:, 0:n_t], in_=rs[:, 0:n_t]), reads=[rs], writes=[rs])
                    for c in range(nchk):
                        TM = tm[c % 2]
                        K.op(dve, lambda c=c, TM=TM, pcs=pcs: nc.vector.tensor_tensor(
                            out=TM[:, 0:n_t], in0=pcs[c][:, 0:n_t], in1=rs[:, 0:n_t], op=ALU.mult),
                            reads=[pcs[c], rs], writes=[TM])
                        K.op(act, lambda c=c, TM=TM, gofs=gofs, cofs=cofs: nc.scalar.activation(
                            out=cn[:, cofs + c, 0:n_t], in_=TM[:, 0:n_t], func=AF.Identity,
                            scale=ag[:, gofs + c:gofs + c + 1]), reads=[TM, ag], writes=[cn])
                pa, pb = ps_next(), ps_next()
                proj_fm(H, n_t, win, 640, 64, NCH, pa)
                proj_fm(H, n_t, win, 704, 64, NCH, pb)
                S = st[sti[0] % 8]
                sti[0] += 1
                rope_combine(tm, pa, pb, RP, 64, n_t, S[0:64, 0:n_t], S)
                K.store(pool, S, [(k2s[:, t0:t0 + n_t], S[0:64, 0:n_t])])
                for h in range(8):
                    ps = ps_next()
                    proj_fm(cn, n_t, wq, h * 128, 128, 3, ps)
                    stage_store(ps, 128, n_t, q1s[h * 128:(h + 1) * 128, t0:t0 + n_t])
                for h in range(8):
                    ps = ps_next()
                    proj_fm(cn, n_t, wkn, h * 128, 128, 2, ps, kb=3)
                    stage_store(ps, 128, n_t, k1s[h * 128:(h + 1) * 128, t0:t0 + n_t])
                for c in range(4):
                    pa, pb = ps_next(), ps_next()
                    proj_fm(cn, n_t, wq, 1024 + c * 128, 128, 3, pa)
                    proj_fm(cn, n_t, wq, 1536 + c * 128, 128, 3, pb)
                    S = st[sti[0] % 8]
                    sti[0] += 1
                    rope_combine(tm, pa, pb, RP, 128, n_t, S[:, 0:n_t], S)
                    K.store(pool, S, [(q2s[c * 128:(c + 1) * 128, t0:t0 + n_t], S[:, 0:n_t])])
                for sub in range(n_t // 128):
                    VS = vst[sub % 2]
                    for cb in range(2):
                        ps = ps_next()
                        for k in range(2):
                            K.op(pe, lambda k=k, cb=cb, ps=ps, sub=sub: nc.tensor.matmul(
                                ps[:, :], cn[:, 3 + k, sub * 128:(sub + 1) * 128], wvv[:, k, cb * 512:(cb + 1) * 512],
                                start=(k == 0), stop=(k == 1)), reads=[cn, wslot[0]], writes=[ps])
                        evac(VS[:, cb * 512:(cb + 1) * 512], VS, ps[:, :], ps)
                    chunk = (t0 + sub * 128) // 128
                    K.store(pool, VS, [(vs[:, :, chunk, :].rearrange("h p d -> p h d"),
                                        VS[:, :].rearrange("p (h d) -> p h d", h=8))])
            K.barrier()
            K.drop(loc)
        with ExitStack() as pes:
            kn = [K.sbuf(f"ckn{i}", [128, T], BF16, pes) for i in range(2)]
            kr = K.sbuf("ckr", [128, T], BF16, pes)
            vh = [K.sbuf(f"cvh{i}", [128, 34, 128], BF16, pes) for i in range(2)]
            qn = [K.sbuf(f"cqn{i}", [128, 512], BF16, pes) for i in range(2)]
            qr = [K.sbuf(f"cqr{i}", [128, 512], BF16, pes) for i in range(2)]
            eb = [K.sbuf(f"ce{i}", [128, 512], BF16, pes) for i in range(6)]
            accs = [K.sbuf(f"cacc{i}", [128, 512], F32, pes) for i in range(2)]
            rc = K.sbuf("crc", [128, 512], F32, pes)
            ob = [K.sbuf(f"cob{i}", [128, 512], BF16, pes) for i in range(2)]
            loc = kn + [kr, rc] + vh + qn + qr + eb + ob + accs
            K.op(dve, lambda: nc.vector.memset(kr[64:128, :], 0.0), writes=[kr])
            for qrb in qr:
                K.op(dve, lambda qrb=qrb: nc.vector.memset(qrb[:, :], 0.0), writes=[qrb])
            K.load(sp, kr, [(kr[0:64, :], k2s)])
            qi = 0
            qtiles = TILES if need_ctx else TILES[1:]
            for h in range(8):
                KN, VH = kn[h % 2], vh[h % 2]
                K.load(sp, KN, [(KN[:, 0:2176], k1s[h * 128:(h + 1) * 128, 0:2176]),
                                (KN[:, 2176:T], k1s[h * 128:(h + 1) * 128, 2176:T])])
                K.load(sp, VH, [(VH[:, :, :], vs[h])])
                for (t0, n_q) in qtiles:
                    QN, QR, OB = qn[qi % 2], qr[qi % 2], ob[qi % 2]
                    pso, psd = PS[4 + qi % 2], PS[6 + qi % 2]
                    qi += 1
                    K.load(sp, QN, [(QN[:, 0:n_q], q1s[h * 128:(h + 1) * 128, t0:t0 + n_q])])
                    K.load(sp, QR, [(QR[0:64, 0:n_q], q2s[h * 64:(h + 1) * 64, t0:t0 + n_q])])
                    chunks = list(range(2)) if t0 < CTX else list(range(34))

                    def qk_fn(c, ps, KN=KN, QN=QN, QR=QR, n_q=n_q):
                        K.op(pe, lambda: nc.tensor.matmul(ps[:, 0:n_q], KN[:, c * 128:(c + 1) * 128], QN[:, 0:n_q],
                                                          start=True, stop=False), reads=[KN, QN], writes=[ps])
                        K.op(pe, lambda: nc.tensor.matmul(ps[:, 0:n_q], kr[:, c * 128:(c + 1) * 128], QR[:, 0:n_q],
                                                          start=False, stop=True), reads=[kr, QR], writes=[ps])

                    def v_fn(c, VH=VH):
                        return VH[:, c, :], VH

                    attn_inner(n_q, chunks, qk_fn, v_fn, scale, eb, PS[0:4], pso, psd, accs[qi % 2], 128)
                    K.op(dve, lambda psd=psd: nc.vector.reciprocal(out=rc[:, 0:n_q], in_=psd[:, 0:n_q]),
                         reads=[psd], writes=[rc])
                    K.op(dve, lambda pso=pso, OB=OB: nc.vector.tensor_tensor(
                        out=OB[:, 0:n_q], in0=pso[:, 0:n_q], in1=rc[:, 0:n_q], op=ALU.mult),
                        reads=[pso, rc], writes=[OB])
                    K.store(pool, OB, [(osc[h * 128:(h + 1) * 128, t0:t0 + n_q], OB[:, 0:n_q])])
            K.barrier()
            K.drop(loc)
        out_proj_phase(layer, wo, TILES if need_ctx else TILES[1:])

    def diff_mixer(layer, j, need_ctx):
        lam_init = 0.8 - 0.6 * math.exp(-0.3 * layer)
        wq, wqP, wvv = wload(0, [(b_wq, 8, 1024), (b_wqP, 8, 1024), (b_wv, 8, 1024)])
        wk, wkP, wo = wload(1, [(b_wk, 8, 1024), (b_wkP, 8, 1024), (b_wo, 8, 1024)])
        scale = 64.0 ** -0.5
        mes = ExitStack()
        lamb = K.sbuf("lamb", [128, 256], F32, mes)
        lamv = K.sbuf("lamv", [128, 8], F32, mes)
        gsb = K.sbuf("gsb", [128, 1], F32, mes)
        lpr = K.sbuf("lpr", [128, 128], F32, mes)
        K.load(sp, lamb, [(lamb[:, :], b_lam)])
        K.load(sp, gsb, [(gsb[:, :], b_gsub)])
        for m in range(2):
            K.op(dve, lambda m=m: nc.vector.tensor_tensor(out=lpr[:, m * 64:(m + 1) * 64],
                                                          in0=lamb[:, m * 128:m * 128 + 64],
                                                          in1=lamb[:, m * 128 + 64:m * 128 + 128], op=ALU.mult),
                 reads=[lamb], writes=[lpr])
            K.op(dve, lambda m=m: nc.vector.reduce_sum(out=lamv[:, m:m + 1], in_=lpr[:, m * 64:(m + 1) * 64],
                                                       axis=mybir.AxisListType.X), reads=[lpr], writes=[lamv])
        K.op(act, lambda: nc.scalar.activation(out=lamv[:, 4:6], in_=lamv[:, 0:2], func=AF.Exp),
             reads=[lamv], writes=[lamv])
        K.op(dve, lambda: nc.vector.scalar_tensor_tensor(out=lamv[:, 2:3], in0=lamv[:, 5:6], scalar=-lam_init,
                                                         in1=lamv[:, 4:5], op0=ALU.add, op1=ALU.subtract),
             reads=[lamv], writes=[lamv])
        K.op(dve, lambda: nc.vector.tensor_scalar(out=lamv[:, 3:4], in0=gsb[:, 0:1], scalar1=1.0 - lam_init,
                                                  scalar2=None, op0=ALU.mult), reads=[gsb], writes=[lamv])
        if DBG.get("mix_stop", 9) < 1:
            K.barrier()
            K.drop([lamb, gsb])
            mes.close()
            return
        with ExitStack() as pes:
            ht = [K.sbuf(f"bh{i}", [128, NCH, 512], BF16, pes) for i in range(2)]
            rp = [K.sbuf(f"brp{i}", [128, 2, 512], F32, pes) for i in range(2)]
            tm = [K.sbuf(f"btm{i}", [128, 512], F32, pes) for i in range(2)]
            st = [K.sbuf(f"bst{i}", [128, 512], BF16, pes) for i in range(8)]
            vst = [K.sbuf(f"bvs{i}", [128, 1024], BF16, pes) for i in range(2)]
            loc = ht + rp + tm + st + vst
            sti = 0
            for ti, (t0, n_t) in enumerate(TILES):
                H, RP = ht[ti % 2], rp[ti % 2]
                K.load(sp, H, [(H[:, :, 0:n_t], hs_v[:, :, t0:t0 + n_t])])
                K.load(sp, RP, [(RP[:, 0, 0:n_t], rope_d[0, :, t0:t0 + n_t]), (RP[:, 1, 0:n_t], rope_d[1, :, t0:t0 + n_t])])
                for (wa, wb, dst) in ((wq, wqP, q1s), (wk, wkP, k1s)):
                    for h in range(8):
                        pa, pb = ps_next(), ps_next()
                        proj_fm(H, n_t, wa, h * 128, 128, NCH, pa)
                        proj_fm(H, n_t, wb, h * 128, 128, NCH, pb)
                        S = st[sti % 8]
                        sti += 1
                        rope_combine(tm, pa, pb, RP, 128, n_t, S[:, 0:n_t], S)
                        K.store(pool, S, [(dst[h * 128:(h + 1) * 128, t0:t0 + n_t], S[:, 0:n_t])])
                for sub in range(n_t // 128):
                    VS = vst[sub % 2]
                    for cb in range(2):
                        ps = ps_next()
                        for k in range(NCH):
                            K.op(pe, lambda k=k, cb=cb, ps=ps, sub=sub, H=H: nc.tensor.matmul(
                                ps[:, :], H[:, k, sub * 128:(sub + 1) * 128], wvv[:, k, cb * 512:(cb + 1) * 512],
                                start=(k == 0), stop=(k == NCH - 1)), reads=[H, wslot[0]], writes=[ps])
                        evac(VS[:, cb * 512:(cb + 1) * 512], VS, ps[:, :], ps)
                    chunk = (t0 + sub * 128) // 128
                    K.store(pool, VS, [(vs[:, :, chunk, :].rearrange("h p d -> p h d"),
                                        VS[:, :].rearrange("p (h d) -> p h d", h=8))])
            K.barrier()
            K.drop(loc)
        if DBG.get("mix_stop", 9) < 2:
            K.drop([lamb, gsb])
            mes.close()
            return
        with ExitStack() as pes:
            kk = [K.sbuf(f"dk{i}", [128, T], BF16, pes) for i in range(2)]
            vh = [K.sbuf(f"dv{i}", [128, 34, 128], BF16, pes) for i in range(2)]
            qq = [K.sbuf(f"dq{i}", [128, 2, 512], BF16, pes) for i in range(2)]
            eb = [K.sbuf(f"de{i}", [128, 512], BF16, pes) for i in range(6)]
            accs = [K.sbuf(f"dacc{i}", [128, 512], F32, pes) for i in range(2)]
            rc = [K.sbuf(f"drc{i}", [128, 512], F32, pes) for i in range(2)]
            aa = [K.sbuf(f"da{i}", [128, 512], F32, pes) for i in range(2)]
            oo = K.sbuf("doo", [128, 512], F32, pes)
            o2 = K.sbuf("do2", [128, 512], F32, pes)
            ob = [K.sbuf(f"dob{i}", [128, 512], BF16, pes) for i in range(2)]
            loc = kk + vh + qq + eb + rc + aa + [oo, o2] + ob + accs
            qi = 0
            qtiles = (TILES if need_ctx else TILES[1:])[0:DBG.get("qtiles", 9)]
            deferred = []
            for qqb in qq:
                K.op(dve, lambda qqb=qqb: nc.vector.memset(qqb[:, :, :], 0.0), writes=[qqb])
            for h in range(DBG.get("heads", 8)):
                KK, VH = kk[h % 2], vh[h % 2]
                K.load(sp, KK, [(KK[:, 0:2176], k1s[h * 128:(h + 1) * 128, 0:2176]),
                                (KK[:, 2176:T], k1s[h * 128:(h + 1) * 128, 2176:T])])
                K.load(sp, VH, [(VH[:, :, :], vs[h])])
                for (t0, n_q) in qtiles:
                    QQ, OB = qq[qi % 2], ob[qi % 2]
                    qi += 1
                    K.load(sp, QQ, [(QQ[0:64, 0, 0:n_q], q1s[h * 128:h * 128 + 64, t0:t0 + n_q]),
                                    (QQ[64:128, 1, 0:n_q], q1s[h * 128 + 64:(h + 1) * 128, t0:t0 + n_q])])
                    chunks = list(range(2)) if t0 < CTX else list(range(34))
                    for m in range(2):
                        pso, psd = PS[4 + m], PS[6 + m]

                        def qk_fn(c, ps, KK=KK, QQ=QQ, n_q=n_q, m=m):
                            K.op(pe, lambda: nc.tensor.matmul(
                                ps[:, 0:n_q], KK[:, c * 128:(c + 1) * 128],
                                QQ[:, m, 0:n_q], start=True, stop=True),
                                reads=[KK, QQ], writes=[ps])

                        def v_fn(c, VH=VH):
                            return VH[:, c, :], VH

                        attn_inner(n_q, chunks, qk_fn, v_fn, scale, eb, PS[0:4], pso, psd, accs[m], 128)
                        if m == 0:
                            while deferred:
                                deferred.pop(0)()
                        K.op(dve, lambda psd=psd, m=m: nc.vector.reciprocal(out=rc[m][:, 0:n_q], in_=psd[:, 0:n_q]),
                             reads=[psd], writes=[rc[m]])
                        K.op(dve, lambda pso=pso, m=m: nc.vector.tensor_tensor(
                            out=aa[m][:, 0:n_q], in0=pso[:, 0:n_q], in1=rc[m][:, 0:n_q], op=ALU.mult),
                            reads=[pso, rc[m]], writes=[aa[m]])
                    K.op(dve, lambda: nc.vector.scalar_tensor_tensor(
                        out=oo[:, 0:n_q], in0=aa[1][:, 0:n_q], scalar=lamv[:, 2:3], in1=aa[0][:, 0:n_q],
                        op0=ALU.mult, op1=ALU.add), reads=[aa[0], aa[1], lamv], writes=[oo])
                    K.op(pool, lambda: nc.gpsimd.tensor_tensor(out=o2[:, 0:n_q], in0=oo[:, 0:n_q], in1=oo[:, 0:n_q],
                                                               op=ALU.mult), reads=[oo], writes=[o2])

                    def tail(n_q=n_q, OB=OB, h=h, t0=t0):
                        pr = PS[7]
                        K.op(pe, lambda: nc.tensor.matmul(pr[:, 0:n_q], ones_f[:, :], o2[:, 0:n_q], start=True,
                                                          stop=True), reads=[ones_f, o2], writes=[pr])
                        K.op(act, lambda: nc.scalar.activation(out=o2[:, 0:n_q], in_=pr[:, 0:n_q], func=AF.Ln,
                                                               bias=epsb[:, 0:1], scale=1.0 / 128),
                             reads=[pr, epsb], writes=[o2])
                        K.op(act, lambda: nc.scalar.activation(out=o2[:, 0:n_q], in_=o2[:, 0:n_q], func=AF.Exp,
                                                               scale=-0.5), reads=[o2], writes=[o2])
                        K.op(dve, lambda: nc.vector.scalar_tensor_tensor(
                            out=OB[:, 0:n_q], in0=oo[:, 0:n_q], scalar=lamv[:, 3:4], in1=o2[:, 0:n_q],
                            op0=ALU.mult, op1=ALU.mult), reads=[oo, o2, lamv], writes=[OB])
                        K.store(pool, OB, [(osc[h * 128:(h + 1) * 128, t0:t0 + n_q], OB[:, 0:n_q])])

                    deferred.append(tail)
            while deferred:
                deferred.pop(0)()
            K.barrier()
            K.drop(loc)
        out_proj_phase(layer, wo, TILES if need_ctx else TILES[1:])
        K.drop([lamb, gsb])
        mes.close()

    def swa_mixer(layer, j, need_ctx):
        wq, wqP, wkd, wkdP, wvv, wo = wload(0, [(c_wq, 8, 1024), (c_wqP, 8, 1024), (c_wkd, 8, 512),
                                                (c_wkdP, 8, 512), (c_wv, 8, 256), (c_wo, 8, 1024)])
        scale = 64.0 ** -0.5
        mes = ExitStack()
        skr = K.sbuf("skr", [128, 16], F32, mes)
        msk = K.sbuf("msk", [128, 2, 512], BF16, mes)
        K.load(sp, skr, [(skr[:, :], c_sinkr)])
        K.load(pool, msk, [(msk[:, 0, :], c_mask[0]), (msk[:, 1, :], c_mask[1])])
        K.op(act, lambda: nc.scalar.activation(out=skr[:, :], in_=skr[:, :], func=AF.Exp), reads=[skr], writes=[skr])
        if DBG.get("mix_stop", 9) < 1:
            K.barrier()
            K.drop([skr, msk])
            mes.close()
            return
        with ExitStack() as pes:
            ht = [K.sbuf(f"sh{i}", [128, NCH, 512], BF16, pes) for i in range(2)]
            rp = [K.sbuf(f"srp{i}", [128, 2, 512], F32, pes) for i in range(2)]
            tm = [K.sbuf(f"stm{i}", [128, 512], F32, pes) for i in range(2)]
            st = [K.sbuf(f"sst{i}", [128, 512], BF16, pes) for i in range(8)]
            vst = [K.sbuf(f"svs{i}", [128, 256], BF16, pes) for i in range(2)]
            loc = ht + rp + tm + st + vst
            sti = 0
            for ti, (t0, n_t) in enumerate(TILES):
                H, RP = ht[ti % 2], rp[ti % 2]
                K.load(sp, H, [(H[:, :, 0:n_t], hs_v[:, :, t0:t0 + n_t])])
                K.load(sp, RP, [(RP[:, 0, 0:n_t], rope_d[0, :, t0:t0 + n_t]), (RP[:, 1, 0:n_t], rope_d[1, :, t0:t0 + n_t])])
                for (wa, wb, nck, dfn) in ((wq, wqP, 8, lambda c: q1s[c * 128:(c + 1) * 128, t0:t0 + n_t]),
                                           (wkd, wkdP, 4, lambda c: kds[c, :, t0:t0 + n_t])):
                    for c in range(nck):
                        pa, pb = ps_next(), ps_next()
                        proj_fm(H, n_t, wa, c * 128, 128, NCH, pa)
                        proj_fm(H, n_t, wb, c * 128, 128, NCH, pb)
                        S = st[sti % 8]
                        sti += 1
                        rope_combine(tm, pa, pb, RP, 128, n_t, S[:, 0:n_t], S)
                        K.store(pool, S, [(dfn(c), S[:, 0:n_t])])
                for sub in range(n_t // 128):
                    VS = vst[sub % 2]
                    ps = ps_next()
                    for k in range(NCH):
                        K.op(pe, lambda k=k, ps=ps, sub=sub, H=H: nc.tensor.matmul(
                            ps[:, 0:256], H[:, k, sub * 128:(sub + 1) * 128], wvv[:, k, :],
                            start=(k == 0), stop=(k == NCH - 1)), reads=[H, wslot[0]], writes=[ps])
                    evac(VS[:, :], VS, ps[:, 0:256], ps)
                    chunk = (t0 + sub * 128) // 128
                    K.store(pool, VS, [(vcs[:, chunk, :], VS[:, :])])
            K.barrier()
            K.drop(loc)
        if DBG.get("mix_stop", 9) < 2:
            K.drop([skr, msk])
            mes.close()
            return
        with ExitStack() as pes:
            kk = [K.sbuf(f"wk{i}", [128, T], BF16, pes) for i in range(2)]
            vv = K.sbuf("wv", [128, 34, 256], BF16, pes)
            qq = [K.sbuf(f"wq{i}", [128, 4, 4, 128], BF16, pes) for i in range(2)]
            eb = [K.sbuf(f"we{i}", [128, 512], BF16, pes) for i in range(4)]
            rc = K.sbuf("wrc", [64, 512], F32, pes)
            accs = [K.sbuf(f"wacc{i}", [128, 512], F32, pes) for i in range(2)]
            ost = [K.sbuf(f"wos{i}", [64, 4, 4, 128], BF16, pes) for i in range(2)]
            loc = kk + [vv, rc] + qq + eb + ost + accs
            K.load(sp, vv, [(vv[:, 0:17, :], vcs[:, 0:17, :]), (vv[:, 17:34, :], vcs[:, 17:34, :])])
            for zb in kk:
                K.op(dve, lambda zb=zb: nc.vector.memset(zb[64:128, :], 0.0), writes=[zb])
            for zb in qq:
                K.op(dve, lambda zb=zb: nc.vector.memset(zb[:, :, :, :], 0.0), writes=[zb])
            qi = 0
            qtiles = (TILES if need_ctx else TILES[1:])[0:DBG.get("qtiles", 9)]
            pi = 0
            for g in range(DBG.get("heads", 4)):
                KK = kk[g % 2]
                K.load(sp, KK, [(KK[0:64, 0:2176], kds[g, 0:64, 0:2176]), (KK[0:64, 2176:T], kds[g, 0:64, 2176:T])])
                for (t0, n_q) in qtiles:
                    QQ, OST = qq[qi % 2], ost[qi % 2]
                    qi += 1
                    nqb = n_q // 128
                    K.load(sp, QQ, [(QQ[0:64, 0:nqb, hh, :],
                                     q1s[(4 * g + hh) * 64:(4 * g + hh + 1) * 64, t0:t0 + n_q].rearrange(
                                         "p (b q) -> p b q", q=128)) for hh in range(4)])
                    for qb in range(nqb):
                        is_ctx = t0 < CTX
                        if is_ctx:
                            chunks = [(0, None), (1, None)]
                        else:
                            blk = (t0 - CTX) // 128 + qb
                            chunks = [(0, None), (1, None)]
                            if blk > 0:
                                chunks.append((2 + blk - 1, 0))
                            chunks.append((2 + blk, None))
                            if blk < 31:
                                chunks.append((2 + blk + 1, 1))
                        pso, psd = PS[4 + pi % 2], PS[6 + pi % 2]
                        ACC = accs[pi % 2]
                        pi += 1
                        QB = QQ[:, qb, :, :].rearrange("p h q -> p (h q)")

                        def qk_fn(cm, ps, KK=KK, QB=QB, QQ=QQ):
                            c = cm[0]
                            K.op(pe, lambda: nc.tensor.matmul(ps[:, :], KK[:, c * 128:(c + 1) * 128], QB,
                                                              start=True, stop=True), reads=[KK, QQ], writes=[ps])

                        def v_fn(cm, g=g):
                            return vv[:, cm[0], g * 64:(g + 1) * 64], vv

                        def mask_fn(cm):
                            if cm[1] is None:
                                return None
                            return msk, msk[:, cm[1], :]

                        attn_inner(512, chunks, qk_fn, v_fn, scale, eb, PS[0:4], pso, psd, ACC, 64,
                                   mask_fn=mask_fn)
                        for hh in range(4):
                            K.op(dve, lambda psd=psd, g=g, hh=hh: nc.vector.tensor_scalar(
                                out=rc[:, hh * 128:(hh + 1) * 128], in0=psd[0:64, hh * 128:(hh + 1) * 128],
                                scalar1=skr[0:64, g * 4 + hh:g * 4 + hh + 1], scalar2=None, op0=ALU.add),
                                reads=[psd, skr], writes=[rc])
                        K.op(dve, lambda: nc.vector.reciprocal(out=rc[:, :], in_=rc[:, :]), reads=[rc], writes=[rc])
                        K.op(dve, lambda pso=pso, OST=OST, qb=qb: nc.vector.tensor_tensor(
                            out=OST[:, qb, :, :].rearrange("p h q -> p (h q)"), in0=pso[0:64, :], in1=rc[:, :],
                            op=ALU.mult), reads=[pso, rc], writes=[OST])
                    pairs = []
                    for hh in range(4):
                        hd = g * 4 + hh
                        pairs.append((osc[hd * 64:(hd + 1) * 64, t0:t0 + n_q].rearrange("p (b q) -> p b q", q=128),
                                      OST[:, 0:nqb, hh, :]))
                    K.store(pool, OST, pairs)
            K.barrier()
            K.drop(loc)
        out_proj_phase(layer, wo, TILES if need_ctx else TILES[1:])
        K.drop([skr, msk])
        mes.close()

    K.barrier()
    stage = 0

    def snap():
        if debug_x and stage <= NSNAP:
            K.load(sp, dummy, [(dbg[stage - 1, k * 128:(k + 1) * 128, 0:768], xs[k * 128:(k + 1) * 128, 0:768])
                               for k in range(NCH)] +
                   [(dbg[stage - 1, k * 128:(k + 1) * 128, 768:1280], xs[k * 128:(k + 1) * 128, T - 512:T])
                    for k in range(NCH)])
            K.barrier()

    def done():
        return stop_after is not None and stage >= stop_after

    for layer in range(DEPTH):
        last = layer == DEPTH - 1
        if layer not in DBG.get("layers", range(DEPTH)):
            stage += 3
            continue
        ffn_phase(layer, 0, TILES, layer == 0, (layer, 1, False))
        stage += 1
        snap()
        if done():
            break
        kind, j = layer % 3, layer // 3
        if kind == 0:
            mla_mixer(layer, j, not last)
        elif kind == 1:
            diff_mixer(layer, j, not last)
        else:
            swa_mixer(layer, j, not last)
        stage += 1
        snap()
        if done():
            break
        t2 = TILES[1:] if last else TILES
        ffn_phase(layer, 2, t2, False, (0, 0, True) if last else (layer + 1, 0, False))
        stage += 1
        snap()
        if done():
            break
    K.barrier()


def _rope_perm():
    p = np.zeros(64, dtype=np.int64)
    for axis in range(2):
        for half in range(2):
            for jj in range(16):
                p[axis * 32 + half * 16 + jj] = axis * 32 + (1 - half) * 16 + jj
    return p


def _rope_tables():
    t = np.arange(SEQ)
    row, col = t // 64, t % 64
    inv = (10000.0 ** (-np.arange(16, dtype=np.float32) / 16)).astype(np.float32)
    ang = np.stack([row[:, None].astype(np.float32) * inv, col[:, None].astype(np.float32) * inv], axis=1)
    cos = np.cos(ang).astype(np.float32)
    sin = np.sin(ang).astype(np.float32)
    cosT = np.ones((64, T), np.float32)
    sinT = np.zeros((64, T), np.float32)
    for axis in range(2):
        for half in range(2):
            f0 = axis * 32 + half * 16
            cosT[f0:f0 + 16, CTX:] = cos[:, axis, :].T
            sinT[f0:f0 + 16, CTX:] = (-1.0 if half == 0 else 1.0) * sin[:, axis, :].T
    return np.stack([np.concatenate([cosT, cosT], 0), np.concatenate([sinT, sinT], 0)], 0)


def _prep_shared(inp):
    f = lambda a: np.ascontiguousarray(np.asarray(a, dtype=np.float32))
    g_ = lambda k: np.asarray(inp[k])
    sh = {}
    sh["w_mod"] = f(inp["w_mod"])
    sh["b_modT"] = f(g_("b_mod").reshape(DEPTH, 72, 128).transpose(2, 0, 1).reshape(128, DEPTH * 72))
    g = g_("g_norm").reshape(DEPTH, 3, NCH, 128).transpose(3, 0, 1, 2).reshape(128, DEPTH * 3 * NCH)
    gf = g_("g_final").reshape(NCH, 128).T
    sh["g_allT"] = f(np.concatenate([g, gf], axis=1))
    sh["w_ffn_in"] = f(inp["w_ffn_in"])
    sh["w_ffn_out"] = f(inp["w_ffn_out"])
    perm = _rope_perm()
    sh["rope"] = f(_rope_tables())
    a_w_in, a_w_qb, a_w_kvb = g_("a_w_in"), g_("a_w_qb"), g_("a_w_kvb")
    sh["a_win"] = f(np.concatenate([a_w_in, a_w_in[:, :, 640 + perm]], axis=2))
    nope = np.concatenate([np.arange(h * 192, h * 192 + 128) for h in range(8)])
    ropec = np.concatenate([np.arange(h * 192 + 128, h * 192 + 192) for h in range(8)])
    ropeP = np.concatenate([h * 192 + 128 + perm for h in range(8)])
    sh["a_wq"] = f(np.concatenate([a_w_qb[:, :, nope], a_w_qb[:, :, ropec], a_w_qb[:, :, ropeP]], axis=2))
    kn = np.concatenate([np.arange(h * 256, h * 256 + 128) for h in range(8)])
    vv = np.concatenate([np.arange(h * 256 + 128, h * 256 + 256) for h in range(8)])
    sh["a_wkn"] = f(a_w_kvb[:, :, kn])
    sh["a_wv"] = f(a_w_kvb[:, :, vv])
    sh["a_wo"] = f(inp["a_w_o"])
    ag = []
    for jj in range(2):
        ag.append(g_("a_g_q")[jj].reshape(3, 128).T)
        ag.append(g_("a_g_kv")[jj].reshape(2, 128).T)
    sh["a_g"] = f(np.concatenate(ag, axis=1))
    bw = g_("b_w_qkv")[0]
    bperm = np.concatenate([blk * 64 + perm for blk in range(16)])
    sh["b_wq"] = f(bw[:, 0:1024])
    sh["b_wqP"] = f(bw[:, 0:1024][:, bperm])
    sh["b_wk"] = f(bw[:, 1024:2048])
    sh["b_wkP"] = f(bw[:, 1024:2048][:, bperm])
    sh["b_wv"] = f(bw[:, 2048:3072])
    sh["b_wo"] = f(g_("b_w_o")[0])
    sh["b_lam"] = f(np.tile(g_("b_lambda")[0].reshape(1, 256), (128, 1)))
    sh["b_gsub"] = f(g_("b_g_sub")[0].reshape(128, 1))
    cw = g_("c_w_qkv")[0]
    sh["c_wq"] = f(cw[:, 0:1024])
    sh["c_wqP"] = f(cw[:, 0:1024][:, bperm])
    ck = cw[:, 1024:1280]
    kd = np.concatenate([np.concatenate([ck[:, g4 * 64:(g4 + 1) * 64]] * 2, axis=1) for g4 in range(4)], axis=1)
    kdP = np.concatenate([np.concatenate([ck[:, g4 * 64 + perm]] * 2, axis=1) for g4 in range(4)], axis=1)
    sh["c_wkd"] = f(kd)
    sh["c_wkdP"] = f(kdP)
    sh["c_wv"] = f(cw[:, 1280:1536])
    sh["c_wo"] = f(g_("c_w_o")[0])
    sh["c_sinkr"] = f(np.tile(g_("c_sink")[0].reshape(1, 16), (128, 1)))
    jj, ii = np.meshgrid(np.arange(128), np.arange(128), indexing="ij")
    mprev = (jj >= ii).astype(np.float32)
    mnext = (jj <= ii).astype(np.float32)
    sh["c_mask"] = f(np.stack([np.tile(mprev, (1, 4)), np.tile(mnext, (1, 4))], axis=0))
    return sh


def _prep_core(inp, b):
    f = lambda a: np.ascontiguousarray(np.asarray(a, dtype=np.float32))
    m = {}
    m["xT0"] = f(np.concatenate([np.asarray(inp["ctx"][b]).T, np.asarray(inp["x"][b]).T], axis=1))
    cl = np.asarray(inp["c"][b]).reshape(NCH, 128).T
    cc = np.asarray(inp["c_ctx"]).reshape(NCH, 128).T
    m["cvec"] = f(np.stack([cl, cc], axis=-1))
    return m


def run(inp, stop_after=None, debug_x=False, trace=False, n_cores=8):
    nc = build_program(stop_after, debug_x)
    sh = _prep_shared(inp)
    in_maps = []
    for b in range(n_cores):
        m = dict(sh)
        m.update(_prep_core(inp, b))
        in_maps.append(m)
    res = run_bass_kernel_spmd(nc, in_maps, core_ids=list(range(n_cores)), trace=trace)
    return res


def kernel(**inputs):
    res = run(inputs)
    out = np.stack([np.ascontiguousarray(res.results[b]["outT"].T) for b in range(8)], axis=0)
    return out.astype(np.float32)
```

```python
import math
from contextlib import ExitStack

import numpy as np
import concourse.bass as bass
import concourse.mybir as mybir
from concourse.bass_utils import run_bass_kernel_spmd

F32 = mybir.dt.float32
BF16 = mybir.dt.bfloat16
AF = mybir.ActivationFunctionType
ALU = mybir.AluOpType

D = 1024
NCH = 8
CTX = 256
SEQ = 4096
T = CTX + SEQ
DEPTH = 4
DFF = 2816
NFF = DFF // 128
EPS = 1e-6
TILES = [(0, 256)] + [(256 + 512 * i, 512) for i in range(8)]
DBG = {}
LAST = {}


class Eng:
    def __init__(self, K, name, eng):
        self.K, self.name, self.eng = K, name, eng
        self.sem = K.new_sem("e_" + name)
        self.count = 0
        self.known = {}

    def wait(self, deps):
        for sem, val in deps.items():
            if sem is self.sem and self.name == "pe":
                continue
            if self.known.get(sem, 0) >= val:
                continue
            self.eng.wait_ge(sem, val)
            self.known[sem] = val


class DSem:
    def __init__(self, sem):
        self.sem = sem
        self.total = 0


class Buf:
    def __init__(self, K, name, t):
        self.K, self.name, self.t = K, name, t
        self.w = {}
        self.r = {}
        self.ld = None
        self.st = None

    def __getitem__(self, idx):
        return self.t[idx]


def _merge(d, sem, val):
    if d.get(sem, 0) < val:
        d[sem] = val


class Kern:
    def __init__(self, nc, es):
        self.nc, self.es = nc, es
        self.nsem = 0
        self.uid = 0
        self.free_dsems = {}
        self.live_dsems = []
        self.pe = Eng(self, "pe", nc.tensor)
        self.act = Eng(self, "act", nc.scalar)
        self.dve = Eng(self, "dve", nc.vector)
        self.pool = Eng(self, "pool", nc.gpsimd)
        self.sp = Eng(self, "sp", nc.sync)
        self.engs = [self.pe, self.act, self.dve, self.pool, self.sp]

    def new_sem(self, name):
        self.nsem += 1
        assert self.nsem <= 100, "too many semaphores"
        return self.es.enter_context(self.nc.semaphore(name))

    def get_dsem(self, q):
        kind = "sw" if q.name == "pool" else "hw"
        fl = self.free_dsems.setdefault(kind, [])
        if fl:
            d = fl.pop()
        else:
            d = DSem(self.new_sem(f"dma{kind}{self.nsem}"))
            d.kind = kind
        self.live_dsems.append(d)
        return d

    def sbuf(self, name, shape, dt, es=None):
        self.uid += 1
        name = f"{name}_{self.uid}"
        t = (es or self.es).enter_context(self.nc.sbuf_tensor(name, shape, dt))
        return Buf(self, name, t)

    def psum(self, name, shape, dt, es=None):
        t = (es or self.es).enter_context(self.nc.psum_tensor(name, shape, dt))
        return Buf(self, name, t)

    def op(self, e, fn, reads=(), writes=()):
        deps = {}
        for b in reads:
            for s, v in b.w.items():
                _merge(deps, s, v)
        for b in writes:
            for s, v in b.w.items():
                _merge(deps, s, v)
            for s, v in b.r.items():
                _merge(deps, s, v)
        e.wait(deps)
        ins = fn()
        ins.then_inc(e.sem, 1)
        e.count += 1
        tok = (e.sem, e.count)
        for b in writes:
            b.w = {tok[0]: tok[1]}
            b.r = {}
        for b in reads:
            if b not in writes:
                _merge(b.r, tok[0], tok[1])
        return ins

    def load(self, q, buf, pairs):
        if buf.ld is None:
            buf.ld = self.get_dsem(q)
        assert buf.ld.kind == ("sw" if q.name == "pool" else "hw")
        deps = {}
        for s, v in buf.w.items():
            _merge(deps, s, v)
        for s, v in buf.r.items():
            _merge(deps, s, v)
        q.wait(deps)
        for o, i in pairs:
            q.eng.dma_start(out=o, in_=i).then_inc(buf.ld.sem, 16)
            buf.ld.total += 16
        buf.w = {buf.ld.sem: buf.ld.total}
        buf.r = {}

    def store(self, q, buf, pairs):
        if buf.st is None:
            buf.st = self.get_dsem(q)
        assert buf.st.kind == ("sw" if q.name == "pool" else "hw")
        deps = {}
        for s, v in buf.w.items():
            _merge(deps, s, v)
        q.wait(deps)
        for o, i in pairs:
            q.eng.dma_start(out=o, in_=i).then_inc(buf.st.sem, 16)
            buf.st.total += 16
        _merge(buf.r, buf.st.sem, buf.st.total)

    def barrier(self):
        deps = {}
        for e in self.engs:
            if e.count:
                deps[e.sem] = e.count
        for d in self.live_dsems:
            if d.total:
                deps[d.sem] = d.total
        for e in self.engs:
            e.wait(deps)

    def drop(self, bufs):
        for b in bufs:
            for d in (b.ld, b.st):
                if d is not None:
                    self.live_dsems.remove(d)
                    self.free_dsems[d.kind].append(d)
            b.ld = b.st = None


def build_program(stop_after=None, debug_x=False):
    nc = bass.Bass("TRN2", target_bir_lowering=False)
    es = ExitStack()
    with es:
        _build(nc, es, stop_after, debug_x)
    return nc


def _build(nc, es, stop_after, debug_x):
    K = Kern(nc, es)
    LAST['K'] = K
    pe, act, dve, pool, sp = K.pe, K.act, K.dve, K.pool, K.sp

    def din(name, shape, dt=F32):
        return nc.dram_tensor(name, shape, dt, kind="ExternalInput").ap()

    x_in = din("xT0", [D, T])
    cvec = din("cvec", [128, NCH, 2])
    w_mod = din("w_mod", [DEPTH, D, 9 * D])
    b_mod = din("b_modT", [128, DEPTH * 72])
    g_all = din("g_allT", [128, DEPTH * 3 * NCH + NCH])
    w_ffn_in = din("w_ffn_in", [DEPTH, 2, D, 2 * DFF])
    w_ffn_out = din("w_ffn_out", [DEPTH, 2, DFF, D])
    rope_d = din("rope", [2, 128, T])
    a_win = din("a_win", [2, D, 768])
    a_wq = din("a_wq", [2, 384, 2048])
    a_wkn = din("a_wkn", [2, 256, 1024])
    a_wv = din("a_wv", [2, 256, 1024])
    a_wo = din("a_wo", [2, D, D])
    a_g = din("a_g", [128, 10])
    b_wq = din("b_wq", [D, D])
    b_wqP = din("b_wqP", [D, D])
    b_wk = din("b_wk", [D, D])
    b_wkP = din("b_wkP", [D, D])
    b_wv = din("b_wv", [D, D])
    b_wo = din("b_wo", [D, D])
    b_lam = din("b_lam", [128, 256])
    b_gsub = din("b_gsub", [128, 1])
    c_wq = din("c_wq", [D, D])
    c_wqP = din("c_wqP", [D, D])
    c_wkd = din("c_wkd", [D, 512])
    c_wkdP = din("c_wkdP", [D, 512])
    c_wv = din("c_wv", [D, 256])
    c_wo = din("c_wo", [D, D])
    c_sinkr = din("c_sinkr", [128, 16])
    c_mask = din("c_mask", [2, 128, 512])
    out_d = nc.dram_tensor("outT", [D, SEQ], F32, kind="ExternalOutput").ap()
    NSNAP = stop_after if stop_after else 12
    if debug_x:
        dbg = nc.dram_tensor("dbg", [NSNAP, D, 1280], F32, kind="ExternalOutput").ap()

    def scratch(name, shape, dt=BF16):
        return nc.dram_tensor(name, shape, dt, kind="Internal").ap()

    q1s = scratch("q1s", [D, T])
    q2s = scratch("q2s", [512, T])
    k1s = scratch("k1s", [D, T])
    k2s = scratch("k2s", [64, T])
    kds = scratch("kds", [4, 128, T])
    vs = scratch("vs", [8, 128, 34, 128])
    vcs = scratch("vcs", [128, 34, 256])
    osc = scratch("osc", [D, T])
    xs = nc.dram_tensor("xs", [D, T], F32, kind="Internal").ap()
    hs = nc.dram_tensor("hs", [D, T], BF16, kind="Internal").ap()

    xs_v = xs.rearrange("(k p) t -> p k t", p=128)
    xin_v = x_in.rearrange("(k p) t -> p k t", p=128)
    hs_v = hs.rearrange("(k p) t -> p k t", p=128)

    WSLOT = 34 * 1024
    wslot = [K.sbuf(f"wslot{i}", [128, WSLOT], BF16) for i in range(2)]
    MOD = K.sbuf("MOD", [128, DEPTH * 9 * NCH * 2], F32)
    GALL = K.sbuf("GALL", [128, DEPTH * 3 * NCH + NCH], F32)
    GS = K.sbuf("GS", [128, DEPTH * 3 * NCH * 2], F32)
    GT = K.sbuf("GT", [128, DEPTH * 3 * NCH * 2], F32)
    ones_f = K.sbuf("ones_f", [128, 128], F32)
    ones_b = K.sbuf("ones_b", [128, 128], BF16)
    epsb = K.sbuf("epsb", [128, 1], F32)
    PS = [K.psum(f"ps{i}", [128, 512], F32) for i in range(8)]

    K.op(dve, lambda: nc.vector.memset(ones_f[:, :], 1.0), writes=[ones_f])
    K.op(dve, lambda: nc.vector.memset(ones_b[:, :], 1.0), writes=[ones_b])
    K.op(dve, lambda: nc.vector.memset(epsb[:, :], EPS), writes=[epsb])

    def mod_col(layer, m, k, lc):
        i = ((layer * 9 + m) * NCH + k) * 2 + lc
        return MOD[:, i:i + 1]

    def gs_col(layer, n, k, lc):
        i = ((layer * 3 + n) * NCH + k) * 2 + lc
        return GS[:, i:i + 1]

    def gt_col(layer, n, k, lc):
        i = ((layer * 3 + n) * NCH + k) * 2 + lc
        return GT[:, i:i + 1]

    dummy = K.sbuf("dmydst", [128, 2], F32)
    K.load(sp, dummy, [(xs[k * 128:(k + 1) * 128, :], x_in[k * 128:(k + 1) * 128, :]) for k in range(NCH)])

    with ExitStack() as pes:
        cv = K.sbuf("cv", [128, NCH, 2], F32, pes)
        sc = K.sbuf("scv", [128, NCH, 2], F32, pes)
        bm = K.sbuf("bm", [128, DEPTH * 72], F32, pes)
        wm = [K.sbuf(f"wm{i}", [128, NCH, 1024], F32, pes) for i in range(2)]
        K.load(sp, cv, [(cv[:, :, :], cvec)])
        K.load(sp, bm, [(bm[:, :], b_mod)])
        K.load(sp, GALL, [(GALL[:, :], g_all)])
        K.op(act, lambda: nc.scalar.activation(out=sc[:, :, :], in_=cv[:, :, :], func=AF.Silu),
             reads=[cv], writes=[sc])
        it = 0
        for layer in range(DEPTH):
            mp = PS[layer % 2]
            for m in range(9):
                wb = wm[it % 2]
                it += 1
                src = w_mod[layer, :, m * 1024:(m + 1) * 1024].rearrange("(k p) n -> p k n", p=128)
                K.load(sp, wb, [(wb[:, 0:4, :], src[:, 0:4, :]), (wb[:, 4:8, :], src[:, 4:8, :])])
                for c in range(NCH):
                    o = (m * NCH + c) * 2
                    for k in range(NCH):
                        K.op(pe, lambda k=k, c=c, o=o, wb=wb, mp=mp: nc.tensor.matmul(
                            mp[:, o:o + 2], wb[:, k, c * 128:(c + 1) * 128], sc[:, k, :],
                            start=(k == 0), stop=(k == NCH - 1)),
                            reads=[wb, sc], writes=[mp])
            base = layer * 144
            for lc in range(2):
                K.op(dve, lambda lc=lc, mp=mp, base=base, layer=layer: nc.vector.tensor_tensor(
                    out=MOD[:, base + lc:base + 144:2], in0=mp[:, lc:144:2],
                    in1=bm[:, layer * 72:(layer + 1) * 72], op=ALU.add),
                    reads=[mp, bm], writes=[MOD])
        for layer in range(DEPTH):
            for n in range(3):
                for lc in range(2):
                    gsl = slice(((layer * 3 + n) * NCH) * 2 + lc, ((layer * 3 + n + 1) * NCH) * 2, 2)
                    scl = slice(((layer * 9 + 3 * n + 1) * NCH) * 2 + lc, ((layer * 9 + 3 * n + 2) * NCH) * 2, 2)
                    gtl = slice(((layer * 9 + 3 * n + 2) * NCH) * 2 + lc, ((layer * 9 + 3 * n + 3) * NCH) * 2, 2)
                    gin = GALL[:, (layer * 3 + n) * NCH:(layer * 3 + n + 1) * NCH]
                    K.op(dve, lambda gsl=gsl, scl=scl, gin=gin: nc.vector.scalar_tensor_tensor(
                        out=GS[:, gsl], in0=MOD[:, scl], scalar=1.0, in1=gin, op0=ALU.add, op1=ALU.mult),
                        reads=[MOD, GALL], writes=[GS])
                    K.op(dve, lambda gsl=gsl, gtl=gtl, n=n: nc.vector.tensor_scalar(
                        out=GT[:, gsl], in0=MOD[:, gtl], scalar1=(1.0 if n == 1 else 0.5), scalar2=None,
                        op0=ALU.mult),
                        reads=[MOD], writes=[GT])
        K.barrier()
        K.drop([cv, bm] + wm)

    def norm_bufs(pes, final, ntm=2):
        nb = {}
        nb["sq"] = [K.sbuf(f"fnsq{i}", [128, 512], F32, pes) for i in range(2)]
        nb["acc"] = K.sbuf("fnacc", [128, 512], F32, pes)
        nb["rs"] = K.sbuf("fnrs", [128, 512], F32, pes)
        nb["tm"] = [K.sbuf(f"fntm{i}", [128, 512], F32, pes) for i in range(ntm)]
        if final:
            nb["ho"] = [K.sbuf(f"fnho{i}", [128, 512], F32, pes) for i in range(4)]
        else:
            nb["ho"] = [K.sbuf("fnho", [128, NCH, 512], BF16, pes)]
        nb["all"] = nb["sq"] + [nb["acc"], nb["rs"]] + nb["tm"] + nb["ho"]
        return nb

    def norm_sq_chunk(XT, k, n_t, nb):
        ACC = nb["acc"]
        dst = ACC if k == 0 else nb["sq"][k % 2]
        K.op(dve, lambda: nc.vector.tensor_tensor(
            out=dst[:, 0:n_t], in0=XT[:, k, 0:n_t], in1=XT[:, k, 0:n_t], op=ALU.mult), reads=[XT], writes=[dst])
        if k > 0:
            K.op(pool, lambda: nc.gpsimd.tensor_tensor(
                out=ACC[:, 0:n_t], in0=ACC[:, 0:n_t], in1=dst[:, 0:n_t], op=ALU.add),
                reads=[dst], writes=[ACC])

    def norm_fused(XT, t0, n_t, nb, ps, nxt, do_part1=True):
        layer, n, final = nxt
        lc = 1 if t0 < CTX else 0
        ACC, RS = nb["acc"], nb["rs"]
        for k in (range(NCH) if do_part1 else []):
            dst = ACC if k == 0 else nb["sq"][k % 2]
            K.op(pool, lambda k=k, dst=dst: nc.gpsimd.tensor_tensor(
                out=dst[:, 0:n_t], in0=XT[:, k, 0:n_t], in1=XT[:, k, 0:n_t], op=ALU.mult), reads=[XT], writes=[dst])
            if k > 0:
                K.op(pool, lambda dst=dst: nc.gpsimd.tensor_tensor(
                    out=ACC[:, 0:n_t], in0=ACC[:, 0:n_t], in1=dst[:, 0:n_t], op=ALU.add),
                    reads=[dst], writes=[ACC])

        def step0():
            K.op(pe, lambda: nc.tensor.matmul(ps[:, 0:n_t], ones_f[:, :], ACC[:, 0:n_t], start=True, stop=True),
                 reads=[ones_f, ACC], writes=[ps])
            K.op(act, lambda: nc.scalar.activation(out=RS[:, 0:n_t], in_=ps[:, 0:n_t], func=AF.Sqrt,
                                                   bias=epsb[:, 0:1], scale=1.0 / D),
                 reads=[ps, epsb], writes=[RS])
            K.op(dve, lambda: nc.vector.reciprocal(out=RS[:, 0:n_t], in_=RS[:, 0:n_t]), reads=[RS], writes=[RS])

        def stepk(k):
            TM = nb["tm"][k % len(nb["tm"])]
            K.op(dve, lambda: nc.vector.tensor_tensor(
                out=TM[:, 0:n_t], in0=XT[:, k, 0:n_t], in1=RS[:, 0:n_t], op=ALU.mult),
                reads=[XT, RS], writes=[TM])
            if final:
                gcol = GALL[:, DEPTH * 3 * NCH + k:DEPTH * 3 * NCH + k + 1]
                HK = nb["ho"][k % 4]
                K.op(act, lambda: nc.scalar.activation(
                    out=HK[:, 0:n_t], in_=TM[:, 0:n_t], func=AF.Identity, scale=gcol),
                    reads=[TM, GALL], writes=[HK])
                K.store(pool, HK, [(out_d[k * 128:(k + 1) * 128, t0 - CTX:t0 - CTX + n_t], HK[:, 0:n_t])])
            else:
                HO = nb["ho"][0]
                K.op(act, lambda: nc.scalar.activation(
                    out=HO[:, k, 0:n_t], in_=TM[:, 0:n_t], func=AF.Identity,
                    scale=gs_col(layer, n, k, lc), bias=mod_col(layer, 3 * n, k, lc)),
                    reads=[TM, GS, MOD], writes=[HO])

        def stepend():
            if not final:
                HO = nb["ho"][0]
                K.store(pool, HO, [(hs_v[:, :, t0:t0 + n_t], HO[:, :, 0:n_t])])

        return [step0] + [(lambda k=k: stepk(k)) for k in range(NCH)] + [stepend]

    def norm_phase(layer, n, tiles, final=False):
        with ExitStack() as pes:
            xt = [K.sbuf(f"nx{i}", [128, NCH, 512], F32, pes) for i in range(2)]
            sq = [K.sbuf(f"nsq{i}", [128, 512], F32, pes) for i in range(2)]
            acc = [K.sbuf(f"nacc{i}", [128, 512], F32, pes) for i in range(2)]
            rs = [K.sbuf(f"nrs{i}", [128, 512], F32, pes) for i in range(2)]
            tmp = [K.sbuf(f"ntmp{i}", [128, 512], F32, pes) for i in range(2)]
            if final:
                ho = [K.sbuf(f"nho{i}", [128, 512], F32, pes) for i in range(4)]
            else:
                ho = [K.sbuf(f"nho{i}", [128, NCH, 512], BF16, pes) for i in range(2)]
            loc = xt + sq + acc + rs + tmp + ho
            for ti, (t0, n_t) in enumerate(tiles):
                lc = 1 if t0 < CTX else 0
                X, SQ, ACC, RS, HO = xt[ti % 2], sq[ti % 2], acc[ti % 2], rs[ti % 2], ho[ti % 2]
                ps = PS[ti % 2]
                K.load(sp, X, [(X[:, 0:4, 0:n_t], xs_v[:, 0:4, t0:t0 + n_t]),
                               (X[:, 4:8, 0:n_t], xs_v[:, 4:8, t0:t0 + n_t])])
                for k in range(NCH):
                    dst = ACC if k == 0 else SQ
                    K.op(act, lambda k=k, dst=dst, X=X: nc.scalar.activation(
                        out=dst[:, 0:n_t], in_=X[:, k, 0:n_t], func=AF.Square),
                        reads=[X], writes=[dst])
                    if k > 0:
                        K.op(pool, lambda ACC=ACC, SQ=SQ: nc.gpsimd.tensor_tensor(
                            out=ACC[:, 0:n_t], in0=ACC[:, 0:n_t], in1=SQ[:, 0:n_t], op=ALU.add),
                            reads=[SQ], writes=[ACC])
                K.op(pe, lambda ps=ps, ACC=ACC: nc.tensor.matmul(
                    ps[:, 0:n_t], ones_f[:, :], ACC[:, 0:n_t], start=True, stop=True),
                    reads=[ones_f, ACC], writes=[ps])
                K.op(act, lambda ps=ps, RS=RS: nc.scalar.activation(
                    out=RS[:, 0:n_t], in_=ps[:, 0:n_t], func=AF.Sqrt, bias=epsb[:, 0:1], scale=1.0 / D),
                    reads=[ps, epsb], writes=[RS])
                K.op(dve, lambda RS=RS: nc.vector.reciprocal(out=RS[:, 0:n_t], in_=RS[:, 0:n_t]),
                     reads=[RS], writes=[RS])
                for k in range(NCH):
                    TM = tmp[k % 2]
                    K.op(dve, lambda k=k, TM=TM, X=X, RS=RS: nc.vector.tensor_tensor(
                        out=TM[:, 0:n_t], in0=X[:, k, 0:n_t], in1=RS[:, 0:n_t], op=ALU.mult),
                        reads=[X, RS], writes=[TM])
                    if final:
                        gcol = GALL[:, DEPTH * 3 * NCH + k:DEPTH * 3 * NCH + k + 1]
                        HK = ho[k % 4]
                        K.op(act, lambda k=k, TM=TM, HK=HK, gcol=gcol: nc.scalar.activation(
                            out=HK[:, 0:n_t], in_=TM[:, 0:n_t], func=AF.Identity, scale=gcol),
                            reads=[TM, GALL], writes=[HK])
                        K.store(pool, HK, [(out_d[k * 128:(k + 1) * 128, t0 - CTX:t0 - CTX + n_t], HK[:, 0:n_t])])
                    else:
                        K.op(act, lambda k=k, TM=TM, HO=HO: nc.scalar.activation(
                            out=HO[:, k, 0:n_t], in_=TM[:, 0:n_t], func=AF.Identity,
                            scale=gs_col(layer, n, k, lc), bias=mod_col(layer, 3 * n, k, lc)),
                            reads=[TM, GS, MOD], writes=[HO])
                if not final:
                    K.store(pool, HO, [(hs_v[:, :, t0:t0 + n_t], HO[:, :, 0:n_t])])
            K.barrier()
            K.drop(loc)

    HH = NFF // 2
    HW = HH * 128

    def ffn_weight_load(layer, j, half, slot):
        W = wslot[slot]
        win = w_ffn_in[layer, j].rearrange("(k p) n -> p k n", p=128)
        wv = W[:, 0:NCH * 2 * HW].rearrange("p (k n) -> p k n", k=NCH)
        pairs = []
        for kk in range(0, NCH, 2):
            pairs.append((wv[:, kk:kk + 2, 0:HW], win[:, kk:kk + 2, half * HW:(half + 1) * HW]))
            pairs.append((wv[:, kk:kk + 2, HW:2 * HW], win[:, kk:kk + 2, DFF + half * HW:DFF + (half + 1) * HW]))
        wout = w_ffn_out[layer, j, half * HW:(half + 1) * HW, :].rearrange("(c p) n -> p c n", p=128)
        ov = W[:, NCH * 2 * HW:NCH * 2 * HW + HH * D].rearrange("p (c n) -> p c n", c=HH)
        pairs.append((ov[:, 0:6, :], wout[:, 0:6, :]))
        pairs.append((ov[:, 6:HH, :], wout[:, 6:HH, :]))
        K.load(pool, W, pairs)

    def ffn_half_phase(layer, n, half, slot, tiles, nxt=None):
        W = wslot[slot]
        wv = W[:, 0:NCH * 2 * HW].rearrange("p (k n) -> p k n", k=NCH)
        ov = W[:, NCH * 2 * HW:NCH * 2 * HW + HH * D].rearrange("p (c n) -> p c n", c=HH)
        with ExitStack() as pes:
            ht = [K.sbuf(f"fh{i}", [128, NCH, 512], BF16, pes) for i in range(2)]
            sg = [K.sbuf(f"fs{i}", [128, 512], F32, pes) for i in range(2)]
            if nxt is None:
                gt = [K.sbuf(f"fg{i}", [128, HH, 512], BF16, pes) for i in range(2)]
                xc = [K.sbuf(f"fx{i}", [128, 512], F32, pes) for i in range(4)]
                loc = ht + gt + sg + xc
            else:
                gt = [K.sbuf("fg0", [128, HH, 512], BF16, pes)]
                XT = K.sbuf("fxt", [128, NCH, 512], F32, pes)
                nb = norm_bufs(pes, nxt[2], ntm=1)
                loc = ht + gt + sg + [XT] + nb["all"]
            xi = 0
            pending = []
            for ti, (t0, n_t) in enumerate(tiles):
                lc = 1 if t0 < CTX else 0
                H, G = ht[ti % 2], gt[ti % len(gt)]
                K.load(sp, H, [(H[:, :, 0:n_t], hs_v[:, :, t0:t0 + n_t])])
                xt_loaded = False
                for c in range(HH):
                    pg, pu = PS[(c % 2) * 2], PS[(c % 2) * 2 + 1]
                    for k in range(NCH):
                        K.op(pe, lambda k=k, c=c, pg=pg, H=H: nc.tensor.matmul(
                            pg[:, 0:n_t], wv[:, k, c * 128:(c + 1) * 128], H[:, k, 0:n_t],
                            start=(k == 0), stop=(k == NCH - 1)), reads=[W, H], writes=[pg])
                    for k in range(NCH):
                        K.op(pe, lambda k=k, c=c, pu=pu, H=H: nc.tensor.matmul(
                            pu[:, 0:n_t], wv[:, k, HW + c * 128:HW + (c + 1) * 128], H[:, k, 0:n_t],
                            start=(k == 0), stop=(k == NCH - 1)), reads=[W, H], writes=[pu])
                    S = sg[c % 2]
                    K.op(act, lambda pg=pg, S=S: nc.scalar.activation(
                        out=S[:, 0:n_t], in_=pg[:, 0:n_t], func=AF.Silu), reads=[pg], writes=[S])
                    K.op(dve, lambda c=c, pu=pu, S=S, G=G: nc.vector.tensor_tensor(
                        out=G[:, c, 0:n_t], in0=pu[:, 0:n_t], in1=S[:, 0:n_t], op=ALU.mult),
                        reads=[pu, S], writes=[G])
                    for _ in range(2):
                        if pending:
                            pending.pop(0)()
                    if nxt is not None and not pending and not xt_loaded:
                        K.load(sp, XT, [(XT[:, 0:4, 0:n_t], xs_v[:, 0:4, t0:t0 + n_t]),
                                        (XT[:, 4:8, 0:n_t], xs_v[:, 4:8, t0:t0 + n_t])])
                        xt_loaded = True
                for d in range(NCH):
                    py = PS[4 + d % 2]
                    if nxt is None:
                        XC = xc[xi % 4]
                        xi += 1
                        K.load(sp, XC, [(XC[:, 0:n_t], xs[d * 128:(d + 1) * 128, t0:t0 + n_t])])
                    for c in range(HH):
                        K.op(pe, lambda c=c, d=d, py=py, G=G: nc.tensor.matmul(
                            py[:, 0:n_t], ov[:, c, d * 128:(d + 1) * 128], G[:, c, 0:n_t],
                            start=(c == 0), stop=(c == HH - 1)), reads=[W, G], writes=[py])
                    if nxt is None:
                        K.op(dve, lambda d=d, py=py, XC=XC: nc.vector.scalar_tensor_tensor(
                            out=XC[:, 0:n_t], in0=py[:, 0:n_t], scalar=gt_col(layer, n, d, lc), in1=XC[:, 0:n_t],
                            op0=ALU.mult, op1=ALU.add), reads=[py, GT, XC], writes=[XC])
                        K.store(pool, XC, [(xs[d * 128:(d + 1) * 128, t0:t0 + n_t], XC[:, 0:n_t])])
                    else:
                        K.op(dve, lambda d=d, py=py: nc.vector.scalar_tensor_tensor(
                            out=XT[:, d, 0:n_t], in0=py[:, 0:n_t], scalar=gt_col(layer, n, d, lc),
                            in1=XT[:, d, 0:n_t], op0=ALU.mult, op1=ALU.add), reads=[py, GT, XT], writes=[XT])
                        norm_sq_chunk(XT, d, n_t, nb)
                if nxt is not None:
                    if not nxt[2]:
                        K.store(pool, XT, [(xs_v[:, 0:4, t0:t0 + n_t], XT[:, 0:4, 0:n_t]),
                                           (xs_v[:, 4:8, t0:t0 + n_t], XT[:, 4:8, 0:n_t])])
                    pending = norm_fused(XT, t0, n_t, nb, PS[6], nxt, do_part1=False)
            while pending:
                pending.pop(0)()
            K.barrier()
            K.drop(loc)

    def ffn_phase(layer, n, tiles, do_norm, nxt):
        j = 0 if n == 0 else 1
        ffn_weight_load(layer, j, 0, 0)
        ffn_weight_load(layer, j, 1, 1)
        if do_norm:
            norm_phase(layer, n, tiles)
        ffn_half_phase(layer, n, 0, 0, tiles)
        ffn_half_phase(layer, n, 1, 1, tiles, nxt)

    psrot = [0]

    def ps_next():
        psrot[0] += 1
        return PS[psrot[0] % 8]

    evrot = [0]

    def evac(dst_ap, dst_buf, src_ap, src_buf):
        evrot[0] += 1
        if evrot[0] % 2:
            K.op(act, lambda: nc.scalar.activation(out=dst_ap, in_=src_ap, func=AF.Identity),
                 reads=[src_buf], writes=[dst_buf])
        else:
            K.op(dve, lambda: nc.vector.tensor_copy(out=dst_ap, in_=src_ap),
                 reads=[src_buf], writes=[dst_buf])

    def wload(slot, items):
        W = wslot[slot]
        off = 0
        views, pairs = [], []
        for ap, k, n in items:
            v = W[:, off:off + k * n].rearrange("p (k n) -> p k n", k=k)
            src = ap.rearrange("(k p) n -> p k n", p=128)
            step = max(1, 4096 // n)
            for kk in range(0, k, step):
                k2 = min(k, kk + step)
                pairs.append((v[:, kk:k2, :], src[:, kk:k2, :]))
            views.append(v)
            off += k * n
        assert off <= WSLOT
        K.load(pool, W, pairs)
        return views

    def rope_combine(pes_bufs, A, B, RP, P, n_t, dst_ap, dst_buf):
        t1, t2 = pes_bufs
        K.op(dve, lambda: nc.vector.tensor_tensor(out=t1[0:P, 0:n_t], in0=A[0:P, 0:n_t], in1=RP[0:P, 0, 0:n_t],
                                                  op=ALU.mult), reads=[A, RP], writes=[t1])
        K.op(dve, lambda: nc.vector.tensor_tensor(out=t2[0:P, 0:n_t], in0=B[0:P, 0:n_t], in1=RP[0:P, 1, 0:n_t],
                                                  op=ALU.mult), reads=[B, RP], writes=[t2])
        K.op(pool, lambda: nc.gpsimd.tensor_tensor(out=dst_ap, in0=t1[0:P, 0:n_t], in1=t2[0:P, 0:n_t],
                                                   op=ALU.add), reads=[t1, t2], writes=[dst_buf])

    def proj_fm(H, n_t, wv, c0, M, nk, ps, kb=0):
        for k in range(nk):
            K.op(pe, lambda k=k: nc.tensor.matmul(ps[0:M, 0:n_t], wv[:, k, c0:c0 + M], H[:, kb + k, 0:n_t],
                                                  start=(k == 0), stop=(k == nk - 1)),
                 reads=[H] + [wslot[0], wslot[1]], writes=[ps])

    def out_proj_phase(layer, wo, tiles):
        osc_v = osc.rearrange("(k p) t -> p k t", p=128)
        nxt = (layer, 2, False)
        with ExitStack() as pes:
            ot = [K.sbuf(f"po{i}", [128, NCH, 512], BF16, pes) for i in range(1)]
            xts = [K.sbuf(f"pxt{i}", [128, NCH, 512], F32, pes) for i in range(2)]
            nb = norm_bufs(pes, False, ntm=1)
            pending = []
            for ti, (t0, n_t) in enumerate(tiles):
                lc = 1 if t0 < CTX else 0
                O = ot[0]
                XT = xts[ti % 2]
                K.load(sp, O, [(O[:, :, 0:n_t], osc_v[:, :, t0:t0 + n_t])])
                K.load(sp, XT, [(XT[:, 0:4, 0:n_t], xs_v[:, 0:4, t0:t0 + n_t]),
                                (XT[:, 4:8, 0:n_t], xs_v[:, 4:8, t0:t0 + n_t])])
                for d in range(NCH):
                    py = PS[d % 4]
                    proj_fm(O, n_t, wo, d * 128, 128, NCH, py)
                    for _ in range(2):
                        if pending:
                            pending.pop(0)()
                    K.op(dve, lambda d=d, py=py, XT=XT: nc.vector.scalar_tensor_tensor(
                        out=XT[:, d, 0:n_t], in0=py[:, 0:n_t], scalar=gt_col(layer, 1, d, lc), in1=XT[:, d, 0:n_t],
                        op0=ALU.mult, op1=ALU.add), reads=[py, GT, XT], writes=[XT])
                    norm_sq_chunk(XT, d, n_t, nb)
                K.store(pool, XT, [(xs_v[:, 0:4, t0:t0 + n_t], XT[:, 0:4, 0:n_t]),
                                   (xs_v[:, 4:8, t0:t0 + n_t], XT[:, 4:8, 0:n_t])])
                while pending:
                    pending.pop(0)()
                pending = norm_fused(XT, t0, n_t, nb, PS[6], nxt, do_part1=False)
            while pending:
                pending.pop(0)()
            K.barrier()
            K.drop(ot + xts + nb["all"])

    def attn_inner(n_q, chunks, qk_fn, v_fn, scale, e_bufs, ps_s, ps_o, ps_d, acc, dv, mask_fn=None):
        nck = len(chunks)
        LA = len(ps_s) - 1

        def emit_qk(i):
            ps = ps_s[i % len(ps_s)]
            qk_fn(chunks[i], ps)

        def emit_rest(i):
            c = chunks[i]
            ps = ps_s[i % len(ps_s)]
            E = e_bufs[i % len(e_bufs)]
            K.op(act, lambda: nc.scalar.activation(out=E[:, 0:n_q], in_=ps[:, 0:n_q], func=AF.Exp, scale=scale),
                 reads=[ps], writes=[E])
            if mask_fn is not None:
                mk = mask_fn(c)
                if mk is not None:
                    mbuf, map_ = mk
                    K.op(pool, lambda: nc.gpsimd.tensor_tensor(out=E[:, 0:n_q], in0=E[:, 0:n_q], in1=map_,
                                                               op=ALU.mult), reads=[E, mbuf], writes=[E])
            vl, vb = v_fn(c)
            K.op(pe, lambda: nc.tensor.matmul(ps_o[0:dv, 0:n_q], vl, E[:, 0:n_q], start=(i == 0),
                                              stop=(i == nck - 1)), reads=[vb, E], writes=[ps_o])
            K.op(pe, lambda: nc.tensor.matmul(ps_d[0:dv, 0:n_q], ones_b[:, 0:dv], E[:, 0:n_q], start=(i == 0),
                                              stop=(i == nck - 1)), reads=[ones_b, E], writes=[ps_d])

        for i in range(min(LA, nck)):
            emit_qk(i)
        for i in range(nck):
            emit_rest(i)
            if i + LA < nck:
                emit_qk(i + LA)

    def mla_mixer(layer, j, need_ctx):
        win, wq, wkn, wvv, wo = wload(0, [(a_win[j], 8, 768), (a_wq[j], 3, 2048), (a_wkn[j], 2, 1024),
                                          (a_wv[j], 2, 1024), (a_wo[j], 8, 1024)])
        scale = 192.0 ** -0.5
        with ExitStack() as pes:
            ht = [K.sbuf(f"ah{i}", [128, NCH, 512], BF16, pes) for i in range(2)]
            rp = [K.sbuf(f"arp{i}", [128, 2, 512], F32, pes) for i in range(2)]
            ag = K.sbuf("ag", [128, 10], F32, pes)
            sq = [K.sbuf(f"asq{i}", [128, 512], F32, pes) for i in range(2)]
            acc = K.sbuf("aacc", [128, 512], F32, pes)
            rs = K.sbuf("ars", [128, 512], F32, pes)
            tm = [K.sbuf(f"atm{i}", [128, 512], F32, pes) for i in range(2)]
            cn = K.sbuf("acn", [128, 5, 512], BF16, pes)
            st = [K.sbuf(f"ast{i}", [128, 512], BF16, pes) for i in range(8)]
            vst = [K.sbuf(f"avs{i}", [128, 1024], BF16, pes) for i in range(2)]
            loc = ht + rp + [ag, acc, rs, cn] + sq + tm + st + vst
            K.load(sp, ag, [(ag[:, :], a_g)])
            sti = [0]

            def stage_store(ps, P, n_t, dram_ap):
                S = st[sti[0] % 8]
                sti[0] += 1
                evac(S[0:P, 0:n_t], S, ps[0:P, 0:n_t], ps)
                K.store(pool, S, [(dram_ap, S[0:P, 0:n_t])])

            for ti, (t0, n_t) in enumerate(TILES):
                H, RP = ht[ti % 2], rp[ti % 2]
                K.load(sp, H, [(H[:, :, 0:n_t], hs_v[:, :, t0:t0 + n_t])])
                K.load(sp, RP, [(RP[:, 0, 0:n_t], rope_d[0, :, t0:t0 + n_t]), (RP[:, 1, 0:n_t], rope_d[1, :, t0:t0 + n_t])])
                for (c_lo, nchk, width, gofs, cofs) in ((0, 3, 384, j * 5, 0), (384, 2, 256, j * 5 + 3, 3)):
                    pcs = []
                    for c in range(nchk):
                        ps = ps_next()
                        proj_fm(H, n_t, win, c_lo + c * 128, 128, NCH, ps)
                        pcs.append(ps)
                        dst = acc if c == 0 else sq[c % 2]
                        K.op(act, lambda ps=ps, dst=dst: nc.scalar.activation(
                            out=dst[:, 0:n_t], in_=ps[:, 0:n_t], func=AF.Square), reads=[ps], writes=[dst])
                        if c > 0:
                            K.op(pool, lambda dst=dst: nc.gpsimd.tensor_tensor(
                                out=acc[:, 0:n_t], in0=acc[:, 0:n_t], in1=dst[:, 0:n_t], op=ALU.add),
                                reads=[dst], writes=[acc])
                    pr = ps_next()
                    K.op(pe, lambda pr=pr: nc.tensor.matmul(pr[:, 0:n_t], ones_f[:, :], acc[:, 0:n_t], start=True,
                                                            stop=True), reads=[ones_f, acc], writes=[pr])
                    K.op(act, lambda pr=pr, width=width: nc.scalar.activation(
                        out=rs[:, 0:n_t], in_=pr[:, 0:n_t], func=AF.Sqrt, bias=epsb[:, 0:1], scale=1.0 / width),
                        reads=[pr, epsb], writes=[rs])
                    K.op(dve, lambda: nc.vector.reciprocal(out=rs[:, 0:n_t], in_=rs[:, 0:n_t]), reads=[rs], writes=[rs])
                    for c in range(nchk):
                        TM = tm[c % 2]
                        K.op(dve, lambda c=c, TM=TM, pcs=pcs: nc.vector.tensor_tensor(
                            out=TM[:, 0:n_t], in0=pcs[c][:, 0:n_t], in1=rs[:, 0:n_t], op=ALU.mult),
                            reads=[pcs[c], rs], writes=[TM])
                        K.op(act, lambda c=c, TM=TM, gofs=gofs, cofs=cofs: nc.scalar.activation(
                            out=cn[:, cofs + c, 0:n_t], in_=TM[:, 0:n_t], func=AF.Identity,
                            scale=ag[:, gofs + c:gofs + c + 1]), reads=[TM, ag], writes=[cn])
                pa, pb = ps_next(), ps_next()
                proj_fm(H, n_t, win, 640, 64, NCH, pa)
                proj_fm(H, n_t, win, 704, 64, NCH, pb)
                S = st[sti[0] % 8]
                sti[0] += 1
                rope_combine(tm, pa, pb, RP, 64, n_t, S[0:64, 0:n_t], S)
                K.store(pool, S, [(k2s[:, t0:t0 + n_t], S[0:64, 0:n_t])])
                for h in range(8):
                    ps = ps_next()
                    proj_fm(cn, n_t, wq, h * 128, 128, 3, ps)
                    stage_store(ps, 128, n_t, q1s[h * 128:(h + 1) * 128, t0:t0 + n_t])
                for h in range(8):
                    ps = ps_next()
                    proj_fm(cn, n_t, wkn, h * 128, 128, 2, ps, kb=3)
                    stage_store(ps, 128, n_t, k1s[h * 128:(h + 1) * 128, t0:t0 + n_t])
                for c in range(4):
                    pa, pb = ps_next(), ps_next()
                    proj_fm(cn, n_t, wq, 1024 + c * 128, 128, 3, pa)
                    proj_fm(cn, n_t, wq, 1536 + c * 128, 128, 3, pb)
                    S = st[sti[0] % 8]
                    sti[0] += 1
                    rope_combine(tm, pa, pb, RP, 128, n_t, S[:, 0:n_t], S)
                    K.store(pool, S, [(q2s[c * 128:(c + 1) * 128, t0:t0 + n_t], S[:, 0:n_t])])
                for sub in range(n_t // 128):
                    VS = vst[sub % 2]
                    for cb in range(2):
                        ps = ps_next()
                        for k in range(2):
                            K.op(pe, lambda k=k, cb=cb, ps=ps, sub=sub: nc.tensor.matmul(
                                ps[:, :], cn[:, 3 + k, sub * 128:(sub + 1) * 128], wvv[:, k, cb * 512:(cb + 1) * 512],
                                start=(k == 0), stop=(k == 1)), reads=[cn, wslot[0]], writes=[ps])
                        evac(VS[:, cb * 512:(cb + 1) * 512], VS, ps[:, :], ps)
                    chunk = (t0 + sub * 128) // 128
                    K.store(pool, VS, [(vs[:, :, chunk, :].rearrange("h p d -> p h d"),
                                        VS[:, :].rearrange("p (h d) -> p h d", h=8))])
            K.barrier()
            K.drop(loc)
        with ExitStack() as pes:
            kn = [K.sbuf(f"ckn{i}", [128, T], BF16, pes) for i in range(2)]
            kr = K.sbuf("ckr", [128, T], BF16, pes)
            vh = [K.sbuf(f"cvh{i}", [128, 34, 128], BF16, pes) for i in range(2)]
            qn = [K.sbuf(f"cqn{i}", [128, 512], BF16, pes) for i in range(2)]
            qr = [K.sbuf(f"cqr{i}", [128, 512], BF16, pes) for i in range(2)]
            eb = [K.sbuf(f"ce{i}", [128, 512], BF16, pes) for i in range(6)]
            accs = [K.sbuf(f"cacc{i}", [128, 512], F32, pes) for i in range(2)]
            rc = K.sbuf("crc", [128, 512], F32, pes)
            ob = [K.sbuf(f"cob{i}", [128, 512], BF16, pes) for i in range(2)]
            loc = kn + [kr, rc] + vh + qn + qr + eb + ob + accs
            K.op(dve, lambda: nc.vector.memset(kr[64:128, :], 0.0), writes=[kr])
            for qrb in qr:
                K.op(dve, lambda qrb=qrb: nc.vector.memset(qrb[:, :], 0.0), writes=[qrb])
            K.load(sp, kr, [(kr[0:64, :], k2s)])
            qi = 0
            qtiles = TILES if need_ctx else TILES[1:]
            for h in range(8):
                KN, VH = kn[h % 2], vh[h % 2]
                K.load(sp, KN, [(KN[:, 0:2176], k1s[h * 128:(h + 1) * 128, 0:2176]),
                                (KN[:, 2176:T], k1s[h * 128:(h + 1) * 128, 2176:T])])
                K.load(sp, VH, [(VH[:, :, :], vs[h])])
                for (t0, n_q) in qtiles:
                    QN, QR, OB = qn[qi % 2], qr[qi % 2], ob[qi % 2]
                    pso, psd = PS[4 + qi % 2], PS[6 + qi % 2]
                    qi += 1
                    K.load(sp, QN, [(QN[:, 0:n_q], q1s[h * 128:(h + 1) * 128, t0:t0 + n_q])])
                    K.load(sp, QR, [(QR[0:64, 0:n_q], q2s[h * 64:(h + 1) * 64, t0:t0 + n_q])])
                    chunks = list(range(2)) if t0 < CTX else list(range(34))

                    def qk_fn(c, ps, KN=KN, QN=QN, QR=QR, n_q=n_q):
                        K.op(pe, lambda: nc.tensor.matmul(ps[:, 0:n_q], KN[:, c * 128:(c + 1) * 128], QN[:, 0:n_q],
                                                          start=True, stop=False), reads=[KN, QN], writes=[ps])
                        K.op(pe, lambda: nc.tensor.matmul(ps[:, 0:n_q], kr[:, c * 128:(c + 1) * 128], QR[:, 0:n_q],
                                                          start=False, stop=True), reads=[kr, QR], writes=[ps])

                    def v_fn(c, VH=VH):
                        return VH[:, c, :], VH

                    attn_inner(n_q, chunks, qk_fn, v_fn, scale, eb, PS[0:4], pso, psd, accs[qi % 2], 128)
                    K.op(dve, lambda psd=psd: nc.vector.reciprocal(out=rc[:, 0:n_q], in_=psd[:, 0:n_q]),
                         reads=[psd], writes=[rc])
                    K.op(dve, lambda pso=pso, OB=OB: nc.vector.tensor_tensor(
                        out=OB[:, 0:n_q], in0=pso[:, 0:n_q], in1=rc[:, 0:n_q], op=ALU.mult),
                        reads=[pso, rc], writes=[OB])
                    K.store(pool, OB, [(osc[h * 128:(h + 1) * 128, t0:t0 + n_q], OB[:, 0:n_q])])
            K.barrier()
            K.drop(loc)
        out_proj_phase(layer, wo, TILES if need_ctx else TILES[1:])

    def diff_mixer(layer, j, need_ctx):
        lam_init = 0.8 - 0.6 * math.exp(-0.3 * layer)
        wq, wqP, wvv = wload(0, [(b_wq, 8, 1024), (b_wqP, 8, 1024), (b_wv, 8, 1024)])
        wk, wkP, wo = wload(1, [(b_wk, 8, 1024), (b_wkP, 8, 1024), (b_wo, 8, 1024)])
        scale = 64.0 ** -0.5
        mes = ExitStack()
        lamb = K.sbuf("lamb", [128, 256], F32, mes)
        lamv = K.sbuf("lamv", [128, 8], F32, mes)
        gsb = K.sbuf("gsb", [128, 1], F32, mes)
        lpr = K.sbuf("lpr", [128, 128], F32, mes)
        K.load(sp, lamb, [(lamb[:, :], b_lam)])
        K.load(sp, gsb, [(gsb[:, :], b_gsub)])
        for m in range(2):
            K.op(dve, lambda m=m: nc.vector.tensor_tensor(out=lpr[:, m * 64:(m + 1) * 64],
                                                          in0=lamb[:, m * 128:m * 128 + 64],
                                                          in1=lamb[:, m * 128 + 64:m * 128 + 128], op=ALU.mult),
                 reads=[lamb], writes=[lpr])
            K.op(dve, lambda m=m: nc.vector.reduce_sum(out=lamv[:, m:m + 1], in_=lpr[:, m * 64:(m + 1) * 64],
                                                       axis=mybir.AxisListType.X), reads=[lpr], writes=[lamv])
        K.op(act, lambda: nc.scalar.activation(out=lamv[:, 4:6], in_=lamv[:, 0:2], func=AF.Exp),
             reads=[lamv], writes=[lamv])
        K.op(dve, lambda: nc.vector.scalar_tensor_tensor(out=lamv[:, 2:3], in0=lamv[:, 5:6], scalar=-lam_init,
                                                         in1=lamv[:, 4:5], op0=ALU.add, op1=ALU.subtract),
             reads=[lamv], writes=[lamv])
        K.op(dve, lambda: nc.vector.tensor_scalar(out=lamv[:, 3:4], in0=gsb[:, 0:1], scalar1=1.0 - lam_init,
                                                  scalar2=None, op0=ALU.mult), reads=[gsb], writes=[lamv])
        if DBG.get("mix_stop", 9) < 1:
            K.barrier()
            K.drop([lamb, gsb])
            mes.close()
            return
        with ExitStack() as pes:
            ht = [K.sbuf(f"bh{i}", [128, NCH, 512], BF16, pes) for i in range(2)]
            rp = [K.sbuf(f"brp{i}", [128, 2, 512], F32, pes) for i in range(2)]
            tm = [K.sbuf(f"btm{i}", [128, 512], F32, pes) for i in range(2)]
            st = [K.sbuf(f"bst{i}", [128, 512], BF16, pes) for i in range(8)]
            vst = [K.sbuf(f"bvs{i}", [128, 1024], BF16, pes) for i in range(2)]
            loc = ht + rp + tm + st + vst
            sti = 0
            for ti, (t0, n_t) in enumerate(TILES):
                H, RP = ht[ti % 2], rp[ti % 2]
                K.load(sp, H, [(H[:, :, 0:n_t], hs_v[:, :, t0:t0 + n_t])])
                K.load(sp, RP, [(RP[:, 0, 0:n_t], rope_d[0, :, t0:t0 + n_t]), (RP[:, 1, 0:n_t], rope_d[1, :, t0:t0 + n_t])])
                for (wa, wb, dst) in ((wq, wqP, q1s), (wk, wkP, k1s)):
                    for h in range(8):
                        pa, pb = ps_next(), ps_next()
                        proj_fm(H, n_t, wa, h * 128, 128, NCH, pa)
                        proj_fm(H, n_t, wb, h * 128, 128, NCH, pb)
                        S = st[sti % 8]
                        sti += 1
                        rope_combine(tm, pa, pb, RP, 128, n_t, S[:, 0:n_t], S)
                        K.store(pool, S, [(dst[h * 128:(h + 1) * 128, t0:t0 + n_t], S[:, 0:n_t])])
                for sub in range(n_t // 128):
                    VS = vst[sub % 2]
                    for cb in range(2):
                        ps = ps_next()
                        for k in range(NCH):
                            K.op(pe, lambda k=k, cb=cb, ps=ps, sub=sub, H=H: nc.tensor.matmul(
                                ps[:, :], H[:, k, sub * 128:(sub + 1) * 128], wvv[:, k, cb * 512:(cb + 1) * 512],
                                start=(k == 0), stop=(k == NCH - 1)), reads=[H, wslot[0]], writes=[ps])
                        evac(VS[:, cb * 512:(cb + 1) * 512], VS, ps[:, :], ps)
                    chunk = (t0 + sub * 128) // 128
                    K.store(pool, VS, [(vs[:, :, chunk, :].rearrange("h p d -> p h d"),
                                        VS[:, :].rearrange("p (h d) -> p h d", h=8))])
            K.barrier()
            K.drop(loc)
        if DBG.get("mix_stop", 9) < 2:
            K.drop([lamb, gsb])
            mes.close()
            return
        with ExitStack() as pes:
            kk = [K.sbuf(f"dk{i}", [128, T], BF16, pes) for i in range(2)]
            vh = [K.sbuf(f"dv{i}", [128, 34, 128], BF16, pes) for i in range(2)]
            qq = [K.sbuf(f"dq{i}", [128, 2, 512], BF16, pes) for i in range(2)]
            eb = [K.sbuf(f"de{i}", [128, 512], BF16, pes) for i in range(6)]
            accs = [K.sbuf(f"dacc{i}", [128, 512], F32, pes) for i in range(2)]
            rc = [K.sbuf(f"drc{i}", [128, 512], F32, pes) for i in range(2)]
            aa = [K.sbuf(f"da{i}", [128, 512], F32, pes) for i in range(2)]
            oo = K.sbuf("doo", [128, 512], F32, pes)
            o2 = K.sbuf("do2", [128, 512], F32, pes)
            ob = [K.sbuf(f"dob{i}", [128, 512], BF16, pes) for i in range(2)]
            loc = kk + vh + qq + eb + rc + aa + [oo, o2] + ob + accs
            qi = 0
            qtiles = (TILES if need_ctx else TILES[1:])[0:DBG.get("qtiles", 9)]
            deferred = []
            for qqb in qq:
                K.op(dve, lambda qqb=qqb: nc.vector.memset(qqb[:, :, :], 0.0), writes=[qqb])
            for h in range(DBG.get("heads", 8)):
                KK, VH = kk[h % 2], vh[h % 2]
                K.load(sp, KK, [(KK[:, 0:2176], k1s[h * 128:(h + 1) * 128, 0:2176]),
                                (KK[:, 2176:T], k1s[h * 128:(h + 1) * 128, 2176:T])])
                K.load(sp, VH, [(VH[:, :, :], vs[h])])
                for (t0, n_q) in qtiles:
                    QQ, OB = qq[qi % 2], ob[qi % 2]
                    qi += 1
                    K.load(sp, QQ, [(QQ[0:64, 0, 0:n_q], q1s[h * 128:h * 128 + 64, t0:t0 + n_q]),
                                    (QQ[64:128, 1, 0:n_q], q1s[h * 128 + 64:(h + 1) * 128, t0:t0 + n_q])])
                    chunks = list(range(2)) if t0 < CTX else list(range(34))
                    for m in range(2):
                        pso, psd = PS[4 + m], PS[6 + m]

                        def qk_fn(c, ps, KK=KK, QQ=QQ, n_q=n_q, m=m):
                            K.op(pe, lambda: nc.tensor.matmul(
                                ps[:, 0:n_q], KK[:, c * 128:(c + 1) * 128],
                                QQ[:, m, 0:n_q], start=True, stop=True),
                                reads=[KK, QQ], writes=[ps])

                        def v_fn(c, VH=VH):
                            return VH[:, c, :], VH

                        attn_inner(n_q, chunks, qk_fn, v_fn, scale, eb, PS[0:4], pso, psd, accs[m], 128)
                        if m == 0:
                            while deferred:
                                deferred.pop(0)()
                        K.op(dve, lambda psd=psd, m=m: nc.vector.reciprocal(out=rc[m][:, 0:n_q], in_=psd[:, 0:n_q]),
                             reads=[psd], writes=[rc[m]])
                        K.op(dve, lambda pso=pso, m=m: nc.vector.tensor_tensor(
                            out=aa[m][:, 0:n_q], in0=pso[:, 0:n_q], in1=rc[m][:, 0:n_q], op=ALU.mult),
                            reads=[pso, rc[m]], writes=[aa[m]])
                    K.op(dve, lambda: nc.vector.scalar_tensor_tensor(
                        out=oo[:, 0:n_q], in0=aa[1][:, 0:n_q], scalar=lamv[:, 2:3], in1=aa[0][:, 0:n_q],
                        op0=ALU.mult, op1=ALU.add), reads=[aa[0], aa[1], lamv], writes=[oo])
                    K.op(pool, lambda: nc.gpsimd.tensor_tensor(out=o2[:, 0:n_q], in0=oo[:, 0:n_q], in1=oo[:, 0:n_q],
                                                               op=ALU.mult), reads=[oo], writes=[o2])

                    def tail(n_q=n_q, OB=OB, h=h, t0=t0):
                        pr = PS[7]
                        K.op(pe, lambda: nc.tensor.matmul(pr[:, 0:n_q], ones_f[:, :], o2[:, 0:n_q], start=True,
                                                          stop=True), reads=[ones_f, o2], writes=[pr])
                        K.op(act, lambda: nc.scalar.activation(out=o2[:, 0:n_q], in_=pr[:, 0:n_q], func=AF.Ln,
                                                               bias=epsb[:, 0:1], scale=1.0 / 128),
                             reads=[pr, epsb], writes=[o2])
                        K.op(act, lambda: nc.scalar.activation(out=o2[:, 0:n_q], in_=o2[:, 0:n_q], func=AF.Exp,
                                                               scale=-0.5), reads=[o2], writes=[o2])
                        K.op(dve, lambda: nc.vector.scalar_tensor_tensor(
                            out=OB[:, 0:n_q], in0=oo[:, 0:n_q], scalar=lamv[:, 3:4], in1=o2[:, 0:n_q],
                            op0=ALU.mult, op1=ALU.mult), reads=[oo, o2, lamv], writes=[OB])
                        K.store(pool, OB, [(osc[h * 128:(h + 1) * 128, t0:t0 + n_q], OB[:, 0:n_q])])

                    deferred.append(tail)
            while deferred:
                deferred.pop(0)()
            K.barrier()
            K.drop(loc)
        out_proj_phase(layer, wo, TILES if need_ctx else TILES[1:])
        K.drop([lamb, gsb])
        mes.close()

    def swa_mixer(layer, j, need_ctx):
        wq, wqP, wkd, wkdP, wvv, wo = wload(0, [(c_wq, 8, 1024), (c_wqP, 8, 1024), (c_wkd, 8, 512),
                                                (c_wkdP, 8, 512), (c_wv, 8, 256), (c_wo, 8, 1024)])
        scale = 64.0 ** -0.5
        mes = ExitStack()
        skr = K.sbuf("skr", [128, 16], F32, mes)
        msk = K.sbuf("msk", [128, 2, 512], BF16, mes)
        K.load(sp, skr, [(skr[:, :], c_sinkr)])
        K.load(pool, msk, [(msk[:, 0, :], c_mask[0]), (msk[:, 1, :], c_mask[1])])
        K.op(act, lambda: nc.scalar.activation(out=skr[:, :], in_=skr[:, :], func=AF.Exp), reads=[skr], writes=[skr])
        if DBG.get("mix_stop", 9) < 1:
            K.barrier()
            K.drop([skr, msk])
            mes.close()
            return
        with ExitStack() as pes:
            ht = [K.sbuf(f"sh{i}", [128, NCH, 512], BF16, pes) for i in range(2)]
            rp = [K.sbuf(f"srp{i}", [128, 2, 512], F32, pes) for i in range(2)]
            tm = [K.sbuf(f"stm{i}", [128, 512], F32, pes) for i in range(2)]
            st = [K.sbuf(f"sst{i}", [128, 512], BF16, pes) for i in range(8)]
            vst = [K.sbuf(f"svs{i}", [128, 256], BF16, pes) for i in range(2)]
            loc = ht + rp + tm + st + vst
            sti = 0
            for ti, (t0, n_t) in enumerate(TILES):
                H, RP = ht[ti % 2], rp[ti % 2]
                K.load(sp, H, [(H[:, :, 0:n_t], hs_v[:, :, t0:t0 + n_t])])
                K.load(sp, RP, [(RP[:, 0, 0:n_t], rope_d[0, :, t0:t0 + n_t]), (RP[:, 1, 0:n_t], rope_d[1, :, t0:t0 + n_t])])
                for (wa, wb, nck, dfn) in ((wq, wqP, 8, lambda c: q1s[c * 128:(c + 1) * 128, t0:t0 + n_t]),
                                           (wkd, wkdP, 4, lambda c: kds[c, :, t0:t0 + n_t])):
                    for c in range(nck):
                        pa, pb = ps_next(), ps_next()
                        proj_fm(H, n_t, wa, c * 128, 128, NCH, pa)
                        proj_fm(H, n_t, wb, c * 128, 128, NCH, pb)
                        S = st[sti % 8]
                        sti += 1
                        rope_combine(tm, pa, pb, RP, 128, n_t, S[:, 0:n_t], S)
                        K.store(pool, S, [(dfn(c), S[:, 0:n_t])])
                for sub in range(n_t // 128):
                    VS = vst[sub % 2]
                    ps = ps_next()
                    for k in range(NCH):
                        K.op(pe, lambda k=k, ps=ps, sub=sub, H=H: nc.tensor.matmul(
                            ps[:, 0:256], H[:, k, sub * 128:(sub + 1) * 128], wvv[:, k, :],
                            start=(k == 0), stop=(k == NCH - 1)), reads=[H, wslot[0]], writes=[ps])
                    evac(VS[:, :], VS, ps[:, 0:256], ps)
                    chunk = (t0 + sub * 128) // 128
                    K.store(pool, VS, [(vcs[:, chunk, :], VS[:, :])])
            K.barrier()
            K.drop(loc)
        if DBG.get("mix_stop", 9) < 2:
            K.drop([skr, msk])
            mes.close()
            return
        with ExitStack() as pes:
            kk = [K.sbuf(f"wk{i}", [128, T], BF16, pes) for i in range(2)]
            vv = K.sbuf("wv", [128, 34, 256], BF16, pes)
            qq = [K.sbuf(f"wq{i}", [128, 4, 4, 128], BF16, pes) for i in range(2)]
            eb = [K.sbuf(f"we{i}", [128, 512], BF16, pes) for i in range(4)]
            rc = K.sbuf("wrc", [64, 512], F32, pes)
            accs = [K.sbuf(f"wacc{i}", [128, 512], F32, pes) for i in range(2)]
            ost = [K.sbuf(f"wos{i}", [64, 4, 4, 128], BF16, pes) for i in range(2)]
            loc = kk + [vv, rc] + qq + eb + ost + accs
            K.load(sp, vv, [(vv[:, 0:17, :], vcs[:, 0:17, :]), (vv[:, 17:34, :], vcs[:, 17:34, :])])
            for zb in kk:
                K.op(dve, lambda zb=zb: nc.vector.memset(zb[64:128, :], 0.0), writes=[zb])
            for zb in qq:
                K.op(dve, lambda zb=zb: nc.vector.memset(zb[:, :, :, :], 0.0), writes=[zb])
            qi = 0
            qtiles = (TILES if need_ctx else TILES[1:])[0:DBG.get("qtiles", 9)]
            pi = 0
            for g in range(DBG.get("heads", 4)):
                KK = kk[g % 2]
                K.load(sp, KK, [(KK[0:64, 0:2176], kds[g, 0:64, 0:2176]), (KK[0:64, 2176:T], kds[g, 0:64, 2176:T])])
                for (t0, n_q) in qtiles:
                    QQ, OST = qq[qi % 2], ost[qi % 2]
                    qi += 1
                    nqb = n_q // 128
                    K.load(sp, QQ, [(QQ[0:64, 0:nqb, hh, :],
                                     q1s[(4 * g + hh) * 64:(4 * g + hh + 1) * 64, t0:t0 + n_q].rearrange(
                                         "p (b q) -> p b q", q=128)) for hh in range(4)])
                    for qb in range(nqb):
                        is_ctx = t0 < CTX
                        if is_ctx:
                            chunks = [(0, None), (1, None)]
                        else:
                            blk = (t0 - CTX) // 128 + qb
                            chunks = [(0, None), (1, None)]
                            if blk > 0:
                                chunks.append((2 + blk - 1, 0))
                            chunks.append((2 + blk, None))
                            if blk < 31:
                                chunks.append((2 + blk + 1, 1))
                        pso, psd = PS[4 + pi % 2], PS[6 + pi % 2]
                        ACC = accs[pi % 2]
                        pi += 1
                        QB = QQ[:, qb, :, :].rearrange("p h q -> p (h q)")

                        def qk_fn(cm, ps, KK=KK, QB=QB, QQ=QQ):
                            c = cm[0]
                            K.op(pe, lambda: nc.tensor.matmul(ps[:, :], KK[:, c * 128:(c + 1) * 128], QB,
                                                              start=True, stop=True), reads=[KK, QQ], writes=[ps])

                        def v_fn(cm, g=g):
                            return vv[:, cm[0], g * 64:(g + 1) * 64], vv

                        def mask_fn(cm):
                            if cm[1] is None:
                                return None
                            return msk, msk[:, cm[1], :]

                        attn_inner(512, chunks, qk_fn, v_fn, scale, eb, PS[0:4], pso, psd, ACC, 64,
                                   mask_fn=mask_fn)
                        for hh in range(4):
                            K.op(dve, lambda psd=psd, g=g, hh=hh: nc.vector.tensor_scalar(
                                out=rc[:, hh * 128:(hh + 1) * 128], in0=psd[0:64, hh * 128:(hh + 1) * 128],
                                scalar1=skr[0:64, g * 4 + hh:g * 4 + hh + 1], scalar2=None, op0=ALU.add),
                                reads=[psd, skr], writes=[rc])
                        K.op(dve, lambda: nc.vector.reciprocal(out=rc[:, :], in_=rc[:, :]), reads=[rc], writes=[rc])
                        K.op(dve, lambda pso=pso, OST=OST, qb=qb: nc.vector.tensor_tensor(
                            out=OST[:, qb, :, :].rearrange("p h q -> p (h q)"), in0=pso[0:64, :], in1=rc[:, :],
                            op=ALU.mult), reads=[pso, rc], writes=[OST])
                    pairs = []
                    for hh in range(4):
                        hd = g * 4 + hh
                        pairs.append((osc[hd * 64:(hd + 1) * 64, t0:t0 + n_q].rearrange("p (b q) -> p b q", q=128),
                                      OST[:, 0:nqb, hh, :]))
                    K.store(pool, OST, pairs)
            K.barrier()
            K.drop(loc)
        out_proj_phase(layer, wo, TILES if need_ctx else TILES[1:])
        K.drop([skr, msk])
        mes.close()

    K.barrier()
    stage = 0

    def snap():
        if debug_x and stage <= NSNAP:
            K.load(sp, dummy, [(dbg[stage - 1, k * 128:(k + 1) * 128, 0:768], xs[k * 128:(k + 1) * 128, 0:768])
                               for k in range(NCH)] +
                   [(dbg[stage - 1, k * 128:(k + 1) * 128, 768:1280], xs[k * 128:(k + 1) * 128, T - 512:T])
                    for k in range(NCH)])
            K.barrier()

    def done():
        return stop_after is not None and stage >= stop_after

    for layer in range(DEPTH):
        last = layer == DEPTH - 1
        if layer not in DBG.get("layers", range(DEPTH)):
            stage += 3
            continue
        ffn_phase(layer, 0, TILES, layer == 0, (layer, 1, False))
        stage += 1
        snap()
        if done():
            break
        kind, j = layer % 3, layer // 3
        if kind == 0:
            mla_mixer(layer, j, not last)
        elif kind == 1:
            diff_mixer(layer, j, not last)
        else:
            swa_mixer(layer, j, not last)
        stage += 1
        snap()
        if done():
            break
        t2 = TILES[1:] if last else TILES
        ffn_phase(layer, 2, t2, False, (0, 0, True) if last else (layer + 1, 0, False))
        stage += 1
        snap()
        if done():
            break
    K.barrier()


def _rope_perm():
    p = np.zeros(64, dtype=np.int64)
    for axis in range(2):
        for half in range(2):
            for jj in range(16):
                p[axis * 32 + half * 16 + jj] = axis * 32 + (1 - half) * 16 + jj
    return p


def _rope_tables():
    t = np.arange(SEQ)
    row, col = t // 64, t % 64
    inv = (10000.0 ** (-np.arange(16, dtype=np.float32) / 16)).astype(np.float32)
    ang = np.stack([row[:, None].astype(np.float32) * inv, col[:, None].astype(np.float32) * inv], axis=1)
    cos = np.cos(ang).astype(np.float32)
    sin = np.sin(ang).astype(np.float32)
    cosT = np.ones((64, T), np.float32)
    sinT = np.zeros((64, T), np.float32)
    for axis in range(2):
        for half in range(2):
            f0 = axis * 32 + half * 16
            cosT[f0:f0 + 16, CTX:] = cos[:, axis, :].T
            sinT[f0:f0 + 16, CTX:] = (-1.0 if half == 0 else 1.0) * sin[:, axis, :].T
    return np.stack([np.concatenate([cosT, cosT], 0), np.concatenate([sinT, sinT], 0)], 0)


def _prep_shared(inp):
    f = lambda a: np.ascontiguousarray(np.asarray(a, dtype=np.float32))
    g_ = lambda k: np.asarray(inp[k])
    sh = {}
    sh["w_mod"] = f(inp["w_mod"])
    sh["b_modT"] = f(g_("b_mod").reshape(DEPTH, 72, 128).transpose(2, 0, 1).reshape(128, DEPTH * 72))
    g = g_("g_norm").reshape(DEPTH, 3, NCH, 128).transpose(3, 0, 1, 2).reshape(128, DEPTH * 3 * NCH)
    gf = g_("g_final").reshape(NCH, 128).T
    sh["g_allT"] = f(np.concatenate([g, gf], axis=1))
    sh["w_ffn_in"] = f(inp["w_ffn_in"])
    sh["w_ffn_out"] = f(inp["w_ffn_out"])
    perm = _rope_perm()
    sh["rope"] = f(_rope_tables())
    a_w_in, a_w_qb, a_w_kvb = g_("a_w_in"), g_("a_w_qb"), g_("a_w_kvb")
    sh["a_win"] = f(np.concatenate([a_w_in, a_w_in[:, :, 640 + perm]], axis=2))
    nope = np.concatenate([np.arange(h * 192, h * 192 + 128) for h in range(8)])
    ropec = np.concatenate([np.arange(h * 192 + 128, h * 192 + 192) for h in range(8)])
    ropeP = np.concatenate([h * 192 + 128 + perm for h in range(8)])
    sh["a_wq"] = f(np.concatenate([a_w_qb[:, :, nope], a_w_qb[:, :, ropec], a_w_qb[:, :, ropeP]], axis=2))
    kn = np.concatenate([np.arange(h * 256, h * 256 + 128) for h in range(8)])
    vv = np.concatenate([np.arange(h * 256 + 128, h * 256 + 256) for h in range(8)])
    sh["a_wkn"] = f(a_w_kvb[:, :, kn])
    sh["a_wv"] = f(a_w_kvb[:, :, vv])
    sh["a_wo"] = f(inp["a_w_o"])
    ag = []
    for jj in range(2):
        ag.append(g_("a_g_q")[jj].reshape(3, 128).T)
        ag.append(g_("a_g_kv")[jj].reshape(2, 128).T)
    sh["a_g"] = f(np.concatenate(ag, axis=1))
    bw = g_("b_w_qkv")[0]
    bperm = np.concatenate([blk * 64 + perm for blk in range(16)])
    sh["b_wq"] = f(bw[:, 0:1024])
    sh["b_wqP"] = f(bw[:, 0:1024][:, bperm])
    sh["b_wk"] = f(bw[:, 1024:2048])
    sh["b_wkP"] = f(bw[:, 1024:2048][:, bperm])
    sh["b_wv"] = f(bw[:, 2048:3072])
    sh["b_wo"] = f(g_("b_w_o")[0])
    sh["b_lam"] = f(np.tile(g_("b_lambda")[0].reshape(1, 256), (128, 1)))
    sh["b_gsub"] = f(g_("b_g_sub")[0].reshape(128, 1))
    cw = g_("c_w_qkv")[0]
    sh["c_wq"] = f(cw[:, 0:1024])
    sh["c_wqP"] = f(cw[:, 0:1024][:, bperm])
    ck = cw[:, 1024:1280]
    kd = np.concatenate([np.concatenate([ck[:, g4 * 64:(g4 + 1) * 64]] * 2, axis=1) for g4 in range(4)], axis=1)
    kdP = np.concatenate([np.concatenate([ck[:, g4 * 64 + perm]] * 2, axis=1) for g4 in range(4)], axis=1)
    sh["c_wkd"] = f(kd)
    sh["c_wkdP"] = f(kdP)
    sh["c_wv"] = f(cw[:, 1280:1536])
    sh["c_wo"] = f(g_("c_w_o")[0])
    sh["c_sinkr"] = f(np.tile(g_("c_sink")[0].reshape(1, 16), (128, 1)))
    jj, ii = np.meshgrid(np.arange(128), np.arange(128), indexing="ij")
    mprev = (jj >= ii).astype(np.float32)
    mnext = (jj <= ii).astype(np.float32)
    sh["c_mask"] = f(np.stack([np.tile(mprev, (1, 4)), np.tile(mnext, (1, 4))], axis=0))
    return sh


def _prep_core(inp, b):
    f = lambda a: np.ascontiguousarray(np.asarray(a, dtype=np.float32))
    m = {}
    m["xT0"] = f(np.concatenate([np.asarray(inp["ctx"][b]).T, np.asarray(inp["x"][b]).T], axis=1))
    cl = np.asarray(inp["c"][b]).reshape(NCH, 128).T
    cc = np.asarray(inp["c_ctx"]).reshape(NCH, 128).T
    m["cvec"] = f(np.stack([cl, cc], axis=-1))
    return m


def run(inp, stop_after=None, debug_x=False, trace=False, n_cores=8):
    nc = build_program(stop_after, debug_x)
    sh = _prep_shared(inp)
    in_maps = []
    for b in range(n_cores):
        m = dict(sh)
        m.update(_prep_core(inp, b))
        in_maps.append(m)
    res = run_bass_kernel_spmd(nc, in_maps, core_ids=list(range(n_cores)), trace=trace)
    return res


def kernel(**inputs):
    res = run(inputs)
    out = np.stack([np.ascontiguousarray(res.results[b]["outT"].T) for b in range(8)], axis=0)
    return out.astype(np.float32)
```
